# Optimizing a Trainium2 kernel written in Bass

```python
import math
import jax, jax.numpy as jnp
from jax import lax
import numpy as np

D_MODEL = 4096
BATCH = 32
SEQ = 256
DEPTH = 2
DEC_BATCH = 2
DEC_SEQ = 2048
PAST_LEN = 512

GRID_W = 64
HEAD_DIM = 128
N_HEADS = 16
N_KV_HEADS = 4
KV_GROUP = N_HEADS // N_KV_HEADS
ATTN_W = N_HEADS * HEAD_DIM
KV_W = N_KV_HEADS * HEAD_DIM
Q_BLOCK = 128
ATTN_SCALE = HEAD_DIM ** -0.5
ROPE_THETA = 10000.0
ROPE_PAIRS_AXIS = HEAD_DIM // 4
FNET_W = D_MODEL // 4
FNET_GROUPS = 4
FNET_GROUP_W = FNET_W // FNET_GROUPS
HYENA_W = D_MODEL // 4
HYENA_ORDER = 2
HYENA_SHORT = 3
HYENA_BANDS = 16
HYENA_POS_DIM = 1 + 2 * HYENA_BANDS
HYENA_FFN_W = 64
HYENA_MIN_DECAY = math.log(1e-2) / 0.3
HYENA_MAX_DECAY = math.log(1e-2) / 1.5
N_BRANCHES = 3
IN_SPLITS = (ATTN_W, KV_W, KV_W, ATTN_W, FNET_W, FNET_W, HYENA_W, HYENA_W, HYENA_W, HYENA_W, D_MODEL, D_MODEL, D_MODEL)
IN_W = 2 * ATTN_W + 2 * KV_W + 2 * FNET_W + 4 * HYENA_W + N_BRANCHES * D_MODEL
EPS = 1e-6

kernel_name = "hybrid_gqa_fnet_hyena_diffusion_step"


def in_offsets():
    offs, s = [], 0
    for w in IN_SPLITS[:-1]:
        s += w
        offs.append(s)
    return offs


def rms_norm(x, g):
    x32 = x.astype(jnp.float32)
    y = x32 * lax.rsqrt(jnp.mean(x32 * x32, axis=-1, keepdims=True) + EPS)
    return (y * g.astype(jnp.float32)).astype(x.dtype)


def axial_rope_tables(L):
    rows = L // GRID_W
    row = jnp.repeat(jnp.arange(rows, dtype=jnp.float32), GRID_W)
    col = jnp.tile(jnp.arange(GRID_W, dtype=jnp.float32), rows)
    inv = ROPE_THETA ** (-jnp.arange(ROPE_PAIRS_AXIS, dtype=jnp.float32) / ROPE_PAIRS_AXIS)
    ang = jnp.concatenate([row[:, None] * inv, col[:, None] * inv], axis=-1)
    return jnp.cos(ang), jnp.sin(ang)


def apply_rope(x, cos, sin):
    x32 = x.astype(jnp.float32)
    x1, x2 = x32[..., :HEAD_DIM // 2], x32[..., HEAD_DIM // 2:]
    c, s = cos[None, :, None, :], sin[None, :, None, :]
    return jnp.concatenate([x1 * c - x2 * s, x2 * c + x1 * s], axis=-1).astype(x.dtype)


def block_attention(q, k, v):
    B, L = q.shape[0], q.shape[1]
    nblk = L // Q_BLOCK
    qb = q.astype(jnp.float32).reshape(B, nblk, Q_BLOCK, N_KV_HEADS, KV_GROUP, HEAD_DIM).transpose(1, 0, 2, 3, 4, 5)
    k32, v32 = k.astype(jnp.float32), v.astype(jnp.float32)

    def one_block(qblk):
        s = jnp.einsum('bqkgd,bskd->bkgqs', qblk, k32) * ATTN_SCALE
        p = jax.nn.softmax(s, axis=-1)
        return jnp.einsum('bkgqs,bskd->bqkgd', p, v32)

    o = lax.map(one_block, qb)
    return o.transpose(1, 0, 2, 3, 4, 5).reshape(B, L, ATTN_W).astype(q.dtype)


def fourier_mix(u):
    B, L, W = u.shape
    ug = u.astype(jnp.float32).reshape(B, L, FNET_GROUPS, FNET_GROUP_W)
    f = jnp.fft.fft2(ug, axes=(1, 3), norm='ortho').real
    return f.reshape(B, L, W).astype(u.dtype)


def short_conv(u, w, b):
    L = u.shape[1]
    pad = HYENA_SHORT // 2
    up = jnp.pad(u, ((0, 0), (pad, pad), (0, 0)))
    y = b
    for j in range(HYENA_SHORT):
        y = y + up[:, j:j + L] * w[j]
    return y


def hyena_filters(L, w1, b1, w2, b2, w3, b3, freq):
    f32 = jnp.float32
    t = jnp.arange(L, dtype=f32)[:, None] / L
    bands = jnp.arange(1, HYENA_BANDS + 1, dtype=f32)[None, :]
    feats = jnp.concatenate([t, jnp.cos(2 * math.pi * t * bands), jnp.sin(2 * math.pi * t * bands)], axis=-1)
    fr = freq.astype(f32)
    h = jnp.sin(fr * (feats @ w1.astype(f32) + b1.astype(f32)))
    h = jnp.sin(fr * (h @ w2.astype(f32) + b2.astype(f32)))
    h = (h @ w3.astype(f32) + b3.astype(f32)).reshape(L, HYENA_ORDER, 2, HYENA_W)
    deltas = jnp.abs(jnp.linspace(HYENA_MIN_DECAY, HYENA_MAX_DECAY, HYENA_W, dtype=f32))
    decay = jnp.exp(-t * deltas[None, :])
    h = h * decay[:, None, None, :]
    return h / (jnp.sum(jnp.abs(h), axis=(0, 2), keepdims=True) + EPS)


def bidir_long_conv(z, h_fwd, h_bwd, bias):
    L, W = h_fwd.shape
    g = jnp.concatenate([h_fwd.at[0].add(h_bwd[0]), jnp.zeros((1, W), jnp.float32), h_bwd[:0:-1]], axis=0)
    z32 = z.astype(jnp.float32)
    y = jnp.fft.irfft(jnp.fft.rfft(z32, n=2 * L, axis=1) * jnp.fft.rfft(g, axis=0)[None], n=2 * L, axis=1)[:, :L]
    return (y + z32 * bias.astype(jnp.float32)).astype(z.dtype)


def hyena_branch(hv, hx1, hx2, short_w, short_b, filt, bias):
    u = short_conv(jnp.concatenate([hv, hx1, hx2], axis=-1), short_w, short_b)
    v, x1, x2 = jnp.split(u, 3, axis=-1)
    z = x1 * bidir_long_conv(v, filt[:, 0, 0], filt[:, 0, 1], bias[0])
    z = x2 * bidir_long_conv(z, filt[:, 1, 0], filt[:, 1, 1], bias[1])
    return z


def layer(x, cond, ctx_k, ctx_v, p):
    B, L, _ = x.shape
    dt = x.dtype
    mod = jax.nn.silu(cond.astype(jnp.float32)) @ p['w_mod'].astype(jnp.float32) + p['b_mod'].astype(jnp.float32)
    shift, scale, gate = jnp.split(mod.astype(dt)[:, None, :], 3, axis=-1)
    h = rms_norm(x, p['g_pre']) * (1 + scale) + shift
    proj = h @ p['w_in']
    q, k, v, a_gate, f_in, f_gate, hv, hx1, hx2, h_gate, g_a, g_f, g_h = jnp.split(proj, in_offsets(), axis=-1)
    q = rms_norm(q.reshape(B, L, N_HEADS, HEAD_DIM), p['q_norm'])
    k = rms_norm(k.reshape(B, L, N_KV_HEADS, HEAD_DIM), p['k_norm'])
    v = v.reshape(B, L, N_KV_HEADS, HEAD_DIM)
    if ctx_k is None:
        k_all, v_all = k, v
    else:
        cos, sin = axial_rope_tables(L)
        q = apply_rope(q, cos, sin)
        k_all = jnp.concatenate([apply_rope(k, cos, sin), ctx_k.astype(dt)], axis=1)
        v_all = jnp.concatenate([v, ctx_v.astype(dt)], axis=1)
    attn = block_attention(q, k_all, v_all) * jax.nn.silu(a_gate)
    fnet = fourier_mix(f_in) * jax.nn.silu(f_gate)
    filt = hyena_filters(L, p['hy_ffn_w1'], p['hy_ffn_b1'], p['hy_ffn_w2'], p['hy_ffn_b2'],
                         p['hy_ffn_w3'], p['hy_ffn_b3'], p['hy_sin_freq'])
    hy = hyena_branch(hv, hx1, hx2, p['hy_short_w'], p['hy_short_b'], filt, p['hy_bias']) * jax.nn.silu(h_gate)
    merged = (jax.nn.sigmoid(g_a) * (attn @ p['w_attn_o'])
              + jax.nn.sigmoid(g_f) * (fnet @ p['w_fnet_o'])
              + jax.nn.sigmoid(g_h) * (hy @ p['w_hy_o']))
    out = rms_norm(merged @ p['w_out'], p['g_post'])
    return x + gate * out, k, v


def setup_inputs(seed: int = 0) -> dict:
    key = jax.random.key(seed)
    ks = jax.random.split(key, 27)
    f32 = jnp.float32

    def nrm(k, shape, scale=1.0):
        return jax.random.normal(k, shape, f32) * scale

    return {
        'x_prompt': nrm(ks[0], (BATCH, SEQ, D_MODEL)),
        'x_sample': nrm(ks[1], (DEC_BATCH, DEC_SEQ, D_MODEL)),
        'cache_k': nrm(ks[2], (DEC_BATCH, DEPTH, PAST_LEN, N_KV_HEADS, HEAD_DIM)),
        'cache_v': nrm(ks[3], (DEC_BATCH, DEPTH, PAST_LEN, N_KV_HEADS, HEAD_DIM)),
        'c': nrm(ks[4], (DEC_BATCH, D_MODEL)),
        'c_ctx': nrm(ks[5], (D_MODEL,)),
        'w_mod': nrm(ks[6], (DEPTH, D_MODEL, 3 * D_MODEL), 0.3 * D_MODEL ** -0.5),
        'b_mod': nrm(ks[7], (DEPTH, 3 * D_MODEL), 0.01),
        'g_pre': 1.0 + nrm(ks[8], (DEPTH, D_MODEL), 0.1),
        'w_in': nrm(ks[9], (DEPTH, D_MODEL, IN_W), D_MODEL ** -0.5),
        'q_norm': 1.0 + nrm(ks[10], (DEPTH, HEAD_DIM), 0.1),
        'k_norm': 1.0 + nrm(ks[11], (DEPTH, HEAD_DIM), 0.1),
        'hy_short_w': nrm(ks[12], (DEPTH, HYENA_SHORT, 3 * HYENA_W), HYENA_SHORT ** -0.5),
        'hy_short_b': nrm(ks[13], (DEPTH, 3 * HYENA_W), 0.01),
        'hy_ffn_w1': nrm(ks[14], (DEPTH, HYENA_POS_DIM, HYENA_FFN_W), HYENA_POS_DIM ** -0.5),
        'hy_ffn_b1': nrm(ks[15], (DEPTH, HYENA_FFN_W), 0.1),
        'hy_ffn_w2': nrm(ks[16], (DEPTH, HYENA_FFN_W, HYENA_FFN_W), HYENA_FFN_W ** -0.5),
        'hy_ffn_b2': nrm(ks[17], (DEPTH, HYENA_FFN_W), 0.1),
        'hy_ffn_w3': nrm(ks[18], (DEPTH, HYENA_FFN_W, HYENA_ORDER * 2 * HYENA_W), HYENA_FFN_W ** -0.5),
        'hy_ffn_b3': nrm(ks[19], (DEPTH, HYENA_ORDER * 2 * HYENA_W), 0.01),
        'hy_sin_freq': 1.0 + nrm(ks[20], (DEPTH, HYENA_FFN_W), 0.1),
        'hy_bias': nrm(ks[21], (DEPTH, HYENA_ORDER, HYENA_W), 0.1),
        'w_attn_o': nrm(ks[22], (DEPTH, ATTN_W, D_MODEL), ATTN_W ** -0.5),
        'w_fnet_o': nrm(ks[23], (DEPTH, FNET_W, D_MODEL), FNET_W ** -0.5),
        'w_hy_o': nrm(ks[24], (DEPTH, HYENA_W, D_MODEL), HYENA_W ** -0.5),
        'w_out': nrm(ks[25], (DEPTH, D_MODEL, D_MODEL), D_MODEL ** -0.5),
        'g_post': 1.0 + nrm(ks[26], (DEPTH, D_MODEL), 0.1),
    }


def reference(x_prompt, x_sample, cache_k, cache_v, c, c_ctx, w_mod, b_mod, g_pre, w_in, q_norm, k_norm,
              hy_short_w, hy_short_b, hy_ffn_w1, hy_ffn_b1, hy_ffn_w2, hy_ffn_b2, hy_ffn_w3, hy_ffn_b3,
              hy_sin_freq, hy_bias, w_attn_o, w_fnet_o, w_hy_o, w_out, g_post):
    params = [dict(w_mod=w_mod[l], b_mod=b_mod[l], g_pre=g_pre[l], w_in=w_in[l], q_norm=q_norm[l],
                   k_norm=k_norm[l], hy_short_w=hy_short_w[l], hy_short_b=hy_short_b[l],
                   hy_ffn_w1=hy_ffn_w1[l], hy_ffn_b1=hy_ffn_b1[l], hy_ffn_w2=hy_ffn_w2[l],
                   hy_ffn_b2=hy_ffn_b2[l], hy_ffn_w3=hy_ffn_w3[l], hy_ffn_b3=hy_ffn_b3[l],
                   hy_sin_freq=hy_sin_freq[l], hy_bias=hy_bias[l], w_attn_o=w_attn_o[l],
                   w_fnet_o=w_fnet_o[l], w_hy_o=w_hy_o[l], w_out=w_out[l], g_post=g_post[l])
              for l in range(DEPTH)]

    xp = x_prompt
    ctx_cond = c_ctx[None, :]
    ks_list, vs_list = [], []
    for l in range(DEPTH):
        xp, k_l, v_l = layer(xp, ctx_cond, None, None, params[l])
        ks_list.append(k_l)
        vs_list.append(v_l)
    y_prompt = xp
    new_cache_k = jnp.stack(ks_list, axis=1)
    new_cache_v = jnp.stack(vs_list, axis=1)

    xs = x_sample
    for l in range(DEPTH):
        xs, _, _ = layer(xs, c, cache_k[:, l], cache_v[:, l], params[l])
    y_sample = xs

    return (y_prompt, y_sample, new_cache_k, new_cache_v)
```

```python
import math
import numpy as np
import concourse.bass as bass
import concourse.mybir as mybir
from concourse.bass_utils import run_bass_kernel_spmd

F32 = mybir.dt.float32
BF16 = mybir.dt.bfloat16
AF = mybir.ActivationFunctionType
ALU = mybir.AluOpType

D = 4096
KC = 32
HD = 128
NH = 16
NKV = 4
LP = 256
PAST = 512
IN_W = 23552
EPS = 1e-6
TT_ = 512
C_Q, C_K, C_V, C_AG, C_FI, C_FG, C_HV, C_HX1, C_HX2, C_HG, C_GA, C_GF, C_GH = (
    0, 2048, 2560, 3072, 5120, 6144, 7168, 8192, 9216, 10240, 11264, 15360, 19456)
TWO_PI = 2.0 * math.pi
HY_MIN_DECAY = math.log(1e-2) / 0.3
HY_MAX_DECAY = math.log(1e-2) / 1.5


class _Op:
    __slots__ = ("eng", "fn", "deps", "sig", "dma", "semi", "cnt", "seq")


class Sched:
    ENGS = ("pe", "act", "dve", "pool", "sp")
    NREG = 56
    NBG = 8
    NDMA = NREG + NBG

    def __init__(self):
        self.ops = {e: [] for e in self.ENGS}
        self.last_w = {}
        self.readers = {}
        self.dma_since_barrier = []
        self.last_op = {e: None for e in self.ENGS}
        self.bar_dep = {e: None for e in self.ENGS}
        self.dma_n = 0
        self.bg_n = 0
        self.bg_w = {}
        self.dma_last_on_sem = [None] * self.NDMA
        self.dma_cnt_on_sem = [0] * self.NDMA
        self.all_dma = []
        self.seq = 0

    def add(self, eng, fn, r=(), w=(), dma=False, bg=False):
        op = _Op()
        op.eng, op.fn, op.sig, op.dma = eng, fn, False, dma
        op.seq = self.seq
        self.seq += 1
        deps = set()
        for k in r:
            lw = self.last_w.get(k)
            if lw is not None:
                deps.add(lw)
            lw = self.bg_w.get(k)
            if lw is not None:
                deps.add(lw)
        for k in w:
            lw = self.last_w.get(k)
            if lw is not None:
                deps.add(lw)
            for rd in self.readers.get(k, ()):
                deps.add(rd)
        if self.bar_dep[eng] is not None:
            deps.add(self.bar_dep[eng])
            self.bar_dep[eng] = None
        deps.discard(op)
        op.deps = [d for d in deps if not (d.eng == "pe" and eng == "pe" and not d.dma and not dma)]
        for k in r:
            lst = self.readers.setdefault(k, [])
            if not dma:
                lst[:] = [x for x in lst if x.dma or x.eng != eng]
            lst.append(op)
        if bg:
            for k in w:
                self.bg_w[k] = op
        else:
            for k in w:
                self.last_w[k] = op
                self.readers[k] = []
        if dma:
            if bg:
                s = self.NREG + self.bg_n % self.NBG
                self.bg_n += 1
            else:
                s = self.dma_n % self.NREG
                self.dma_n += 1
            prev = self.dma_last_on_sem[s]
            if prev is not None:
                op.deps.append(prev)
            self.dma_cnt_on_sem[s] += 16
            op.semi, op.cnt = s, self.dma_cnt_on_sem[s]
            self.dma_last_on_sem[s] = op
            if not bg:
                self.dma_since_barrier.append(op)
            self.all_dma.append(op)
        self.ops[eng].append(op)
        self.last_op[eng] = op
        return op

    def barrier(self):
        deps = [self.last_op[e] for e in ("pe", "act", "dve", "pool") if self.last_op[e] is not None]
        seen = {}
        for d in self.dma_since_barrier:
            seen[d.semi] = d
        deps += list(seen.values())
        m = _Op()
        m.eng, m.fn, m.sig, m.dma = "sp", (lambda e: e.nop()), False, False
        m.seq = self.seq
        self.seq += 1
        m.deps = deps
        self.ops["sp"].append(m)
        self.last_op["sp"] = m
        for e in ("pe", "act", "dve", "pool"):
            self.bar_dep[e] = m
        self.last_w = {}
        self.readers = {}
        self.dma_since_barrier = []

    def emit(self, nc, block, eng_sems, dma_sems):
        for e in self.ENGS:
            for op in self.ops[e]:
                for d in op.deps:
                    if not d.dma:
                        d.sig = True
        for e in self.ENGS:
            c = 0
            for op in self.ops[e]:
                if not op.dma and op.sig:
                    c += 1
                    op.cnt = c
        handles = {"pe": block.tensor, "act": block.scalar, "dve": block.vector, "pool": block.gpsimd,
                   "sp": block.sync}
        final_waits = [(s, self.dma_cnt_on_sem[s]) for s in range(self.NDMA) if self.dma_cnt_on_sem[s] > 0]

        def make(ename):
            ops = self.ops[ename]

            def body(e):
                waited = {}
                for op in ops:
                    need = {}
                    for d in op.deps:
                        if d.dma:
                            key, val, sem = ("d", d.semi), d.cnt, dma_sems[d.semi]
                        else:
                            key, val, sem = ("e", d.eng), d.cnt, eng_sems[d.eng]
                        if waited.get(key, 0) >= val:
                            continue
                        if key not in need or need[key][0] < val:
                            need[key] = (val, sem)
                    for key, (val, sem) in need.items():
                        e.wait_ge(sem, val)
                        waited[key] = val
                    ins = op.fn(e)
                    if op.dma:
                        ins.then_inc(dma_sems[op.semi], 16)
                    elif op.sig:
                        ins.then_inc(eng_sems[ename], 1)
                if ename == "sp":
                    for s, v in final_waits:
                        e.wait_ge(dma_sems[s], v)
            return body

        for ename in self.ENGS:
            handles[ename](make(ename))


class Cfg:
    def __init__(self, ncores=4, depth=2, nps=8, ls=2048, debug=False):
        self.ncores, self.depth, self.nps, self.ls, self.debug = ncores, depth, nps, ls, debug
        self.tp = nps * LP
        self.T = self.tp + ls


def _table_names(L):
    return [f"fn_cl{L}", f"fn_nsl{L}", f"hy_cf{L}", f"hy_sf{L}", f"hy_cft{L}", f"hy_sft{L}", f"hy_feat{L}",
            f"hy_tneg{L}"]


def make_tables(ls):
    t = {}
    c = np.arange(256, dtype=np.float64)
    ang = 2 * np.pi * np.outer(c, c) / 256.0
    t["fn_cw"] = np.cos(ang)
    t["fn_sw"] = np.sin(ang)
    for L in sorted({LP, ls}):
        p = np.arange(L, dtype=np.float64)
        a = 2 * np.pi * np.outer(p, p) / L
        sc = 1.0 / math.sqrt(L * 256.0)
        t[f"fn_cl{L}"] = np.cos(a) * sc
        t[f"fn_nsl{L}"] = -np.sin(a) * sc
        w = np.pi * np.outer(p, 2 * p + 1) / (2.0 * L)
        t[f"hy_cf{L}"] = np.cos(w)
        t[f"hy_sf{L}"] = np.sin(w)
        t[f"hy_cft{L}"] = np.cos(w).T / L
        t[f"hy_sft{L}"] = np.sin(w).T / L
        tt = p / L
        bands = np.arange(1, 17, dtype=np.float64)
        feats = np.concatenate([tt[:, None], np.cos(2 * np.pi * tt[:, None] * bands),
                                np.sin(2 * np.pi * tt[:, None] * bands)], axis=1)
        t[f"hy_feat{L}"] = feats.T
        t[f"hy_tneg{L}"] = (-tt).reshape(L // 128, 128).T
    deltas = np.abs(np.linspace(HY_MIN_DECAY, HY_MAX_DECAY, 1024))
    t["hy_delta"] = np.broadcast_to(deltas[None, :], (128, 1024))
    rows = ls // 64
    row = np.repeat(np.arange(rows, dtype=np.float64), 64)
    col = np.tile(np.arange(64, dtype=np.float64), rows)
    inv = 10000.0 ** (-np.arange(32, dtype=np.float64) / 32.0)
    angr = np.concatenate([row[:, None] * inv, col[:, None] * inv], axis=1)
    t["rope_cos"] = np.concatenate([np.cos(angr).T, np.cos(angr).T], axis=0)
    t["rope_sin"] = np.concatenate([np.sin(angr).T, np.sin(angr).T], axis=0)
    rm = np.zeros((128, 128))
    for d in range(64):
        rm[d, d + 64] = 1.0
        rm[d + 64, d] = -1.0
    t["rope_rm"] = rm
    t["ident"] = np.eye(128)
    sel = np.zeros((2, 2, 128))
    sel[0, 0, :] = 1.0
    sel[1, 1, :] = 1.0
    t["sel"] = sel.reshape(2, 256)
    return {k: np.ascontiguousarray(v, dtype=np.float32) for k, v in t.items()}


def build_program(cfg):
    nc = bass.Bass("TRN2", target_bir_lowering=False)
    S = Sched()
    DEPTH, NPS, LS, TP, T = cfg.depth, cfg.nps, cfg.ls, cfg.tp, cfg.T
    Ls = sorted({LP, LS})

    def din(name, shape):
        return nc.dram_tensor(name, list(shape), F32, kind="ExternalInput").ap()

    def dout(name, shape):
        return nc.dram_tensor(name, list(shape), F32, kind="ExternalOutput").ap()

    def dscr(name, shape, dt):
        return nc.dram_tensor(name, list(shape), dt, kind="Internal").ap()

    xp = din("xp", [TP, D])
    xs = din("xs", [LS, D])
    ck = din("ck", [DEPTH, PAST, NKV * HD])
    cv = din("cv", [DEPTH, PAST, NKV * HD])
    cond = din("cond", [2, D])
    w_mod = din("w_mod", [DEPTH, D, 3 * D])
    b_mod = din("b_mod", [DEPTH, 3 * D])
    g_pre = din("g_pre", [DEPTH, D])
    w_in = din("w_in", [DEPTH, D, IN_W])
    q_norm = din("q_norm", [DEPTH, HD])
    k_norm = din("k_norm", [DEPTH, HD])
    hy_sw = din("hy_short_w", [DEPTH, 3, 3072])
    hy_sb = din("hy_short_b", [DEPTH, 3072])
    hy_w1 = din("hy_ffn_w1", [DEPTH, 33, 64])
    hy_b1 = din("hy_ffn_b1", [DEPTH, 64])
    hy_w2 = din("hy_ffn_w2", [DEPTH, 64, 64])
    hy_b2 = din("hy_ffn_b2", [DEPTH, 64])
    hy_w3 = din("hy_ffn_w3", [DEPTH, 64, 4096])
    hy_b3 = din("hy_ffn_b3", [DEPTH, 4096])
    hy_fr = din("hy_sin_freq", [DEPTH, 64])
    hy_bias = din("hy_bias", [DEPTH, 2, 1024])
    w_ao = din("w_attn_o", [DEPTH, 2048, D])
    w_fo = din("w_fnet_o", [DEPTH, 1024, D])
    w_ho = din("w_hy_o", [DEPTH, 1024, D])
    w_out = din("w_out", [DEPTH, D, D])
    g_post = din("g_post", [DEPTH, D])
    tabs = {}
    tshapes = {"fn_cw": [256, 256], "fn_sw": [256, 256], "hy_delta": [128, 1024], "rope_cos": [128, LS],
               "rope_sin": [128, LS], "rope_rm": [128, 128], "ident": [128, 128], "sel": [2, 256]}
    for L in Ls:
        for n in _table_names(L)[:6]:
            tshapes[n] = [L, L]
        tshapes[f"hy_feat{L}"] = [33, L]
        tshapes[f"hy_tneg{L}"] = [128, L // 128]
    for n, sh in tshapes.items():
        tabs[n] = din(n, sh)

    y_p = dout("y_p", [TP, D])
    y_s = dout("y_s", [LS, D])
    nck = dout("nck", [NPS, DEPTH, LP, NKV * HD])
    ncv = dout("ncv", [NPS, DEPTH, LP, NKV * HD])

    wb_mod = [dscr(f"wb_mod{l}", [24, 128, KC, 512], BF16) for l in range(DEPTH)]
    wb_in = [dscr(f"wb_in{l}", [46, 128, KC, 512], BF16) for l in range(DEPTH)]
    wb_o = [dscr(f"wb_o{l}", [8, 128, KC, 512], BF16) for l in range(DEPTH)]
    wb_out = [dscr(f"wb_out{l}", [8, 128, KC, 512], BF16) for l in range(DEPTH)]
    projT = dscr("projT", [IN_W, T], BF16)
    bT = dscr("bT", [D, T], BF16)
    xres = dscr("xres", [T, D], F32)
    ggb = dscr("ggb", [2, 128, D], F32)
    gspec = {L: dscr(f"gspec{L}", [2, 2, L, 1024], BF16) for L in Ls}
    if cfg.debug:
        dbg_proj = nc.dram_tensor("dbg_proj", [IN_W, T], BF16, kind="ExternalOutput").ap()
        dbg_bT = nc.dram_tensor("dbg_bT", [D, T], BF16, kind="ExternalOutput").ap()

    ARENA_W = 45056
    SLOT = 9216
    ctx = []
    arena_t = nc.sbuf_tensor("arena", [128, ARENA_W], F32)
    psum_t = nc.psum_tensor("psum", [128, 8, 512], F32)
    arena = arena_t.__enter__()
    ps = psum_t.__enter__()

    class Arena:
        def __init__(self, base=0):
            self.off = base

        def f32(self, n):
            a = arena[:, self.off:self.off + n]
            self.off += n
            assert self.off <= ARENA_W, f"arena overflow {self.off}"
            return a

        def bf16(self, n):
            assert n % 2 == 0
            a = arena[:, self.off:self.off + n // 2].bitcast(BF16)
            self.off += n // 2
            assert self.off <= ARENA_W, f"arena overflow {self.off}"
            return a

    def psb(b):
        return ps[:, b, :]

    def psb16(b):
        return ps[:, b, :].bitcast(BF16)

    def MM(out, lhsT, rhs, start, stop, r, w):
        S.add("pe", lambda e: e.matmul(out, lhsT, rhs, start=start, stop=stop), r, w)

    def TR(out, in_, ident, r, w):
        S.add("pe", lambda e: e.transpose(out, in_, ident), r, w)

    def ACT(out, in_, func, r, w, bias=None, scale=None, accum=None):
        kw = {}
        if bias is not None:
            kw["bias"] = bias
        if scale is not None:
            kw["scale"] = scale
        if accum is not None:
            kw["accum_out"] = accum
        S.add("act", lambda e: e.activation(out=out, in_=in_, func=func, **kw), r, w)

    def DMA(q, out, in_, r, w, slow=False, bg=False):
        if bg:
            S.add(q, lambda e: e.dma_start(out=out, in_=in_), r, w, dma=True, bg=True)
        elif slow:
            S.add(q, lambda e: e.dma_start(out=out, in_=in_, allow_slow_non_contiguous=True), r, w, dma=True)
        else:
            S.add(q, lambda e: e.dma_start(out=out, in_=in_), r, w, dma=True)

    def TTo(eng, out, a, b, op, r, w):
        S.add(eng, lambda e: e.tensor_tensor(out=out, in0=a, in1=b, op=op), r, w)

    def TS(eng, out, a, s1, s2, op0, op1, r, w):
        if op1 is None:
            S.add(eng, lambda e: e.tensor_scalar(out=out, in0=a, scalar1=s1, scalar2=None, op0=op0), r, w)
        else:
            S.add(eng, lambda e: e.tensor_scalar(out=out, in0=a, scalar1=s1, scalar2=s2, op0=op0, op1=op1), r, w)

    def STT(out, in0, scalar, in1, op0, op1, r, w):
        S.add("dve", lambda e: e.scalar_tensor_tensor(out=out, in0=in0, scalar=scalar, in1=in1, op0=op0, op1=op1),
              r, w)

    def CP(eng, out, in_, r, w):
        if eng == "act":
            S.add("act", lambda e: e.copy(out=out, in_=in_), r, w)
        else:
            S.add(eng, lambda e: e.tensor_copy(out=out, in_=in_), r, w)

    def RECIP(out, in_, r, w):
        S.add("dve", lambda e: e.reciprocal(out=out, in_=in_), r, w)

    def RSQRT_ACT(out, in_, mul, r, w):
        ACT(out, in_, AF.Ln, r, w, bias=EPS_AP(out), scale=mul)
        ACT(out, out, AF.Exp, w, w, scale=-0.5)

    def MSET(eng, ap, val, r, w):
        S.add(eng, lambda e: e.memset(ap, val), r, w)

    PA = Arena(0)
    ident_bf = PA.bf16(128)
    ident_f = PA.f32(128)
    ones_bf = PA.bf16(128)
    ones_f = PA.f32(128)
    rm_bf = PA.bf16(128)
    sel_f = PA.f32(256)
    condT = PA.bf16(64).rearrange("p (k j) -> p k j", j=2)
    modT = PA.f32(128).rearrange("p (k j) -> p k j", j=2)
    qn_col = PA.f32(2 * DEPTH)
    eps_col = PA.f32(1)

    def EPS_AP(out):
        return eps_col[0:out.shape[0], 0:1]
    PERSIST = PA.off
    DMA("pool", ident_bf, tabs["ident"], [], ["ident_bf"])
    DMA("sp", ident_f, tabs["ident"], [], ["ident_f"])
    DMA("pool", rm_bf, tabs["rope_rm"], [], ["rm_bf"])
    DMA("sp", sel_f[0:2, :], tabs["sel"], [], ["sel_f"])
    MSET("dve", ones_bf, 1.0, [], ["ones_bf"])
    MSET("dve", ones_f, 1.0, [], ["ones_f"])
    MSET("dve", eps_col, EPS, [], ["eps_col"])
    for l in range(DEPTH):
        DMA("sp", qn_col[:, 2 * l:2 * l + 1], q_norm[l].rearrange("(d o) -> d o", o=1), [], [f"qnp{l}"], slow=True)
        DMA("sp", qn_col[:, 2 * l + 1:2 * l + 2], k_norm[l].rearrange("(d o) -> d o", o=1), [], [f"knp{l}"], slow=True)
    A0 = Arena(PERSIST)
    cond_f = A0.f32(64).rearrange("p (k j) -> p k j", j=2)
    for j in range(2):
        DMA("sp", cond_f[:, :, j], cond[j].rearrange("(k p) -> p k", p=128), [], [f"cond_f{j}"], slow=True)
    ACT(condT, cond_f, AF.Silu, ["cond_f0", "cond_f1"], ["condT"])

    def cast_weights(l):
        for cb in range(24):
            DMA("pool", wb_mod[l][cb], w_mod[l][:, cb * 512:(cb + 1) * 512].rearrange("(k p) n -> p k n", p=128),
                [], [f"wbmod{l}_{cb}"], bg=True)
        for cb in range(46):
            DMA("pool", wb_in[l][cb], w_in[l][:, cb * 512:(cb + 1) * 512].rearrange("(k p) n -> p k n", p=128),
                [], [f"wbin{l}_{cb}"], bg=True)
        for cb in range(8):
            cs = slice(cb * 512, (cb + 1) * 512)
            DMA("pool", wb_o[l][cb][:, 0:16, :], w_ao[l][:, cs].rearrange("(k p) n -> p k n", p=128), [], [f"wbo{l}_{cb}a"], bg=True)
            DMA("pool", wb_o[l][cb][:, 16:24, :], w_fo[l][:, cs].rearrange("(k p) n -> p k n", p=128), [], [f"wbo{l}_{cb}f"], bg=True)
            DMA("pool", wb_o[l][cb][:, 24:32, :], w_ho[l][:, cs].rearrange("(k p) n -> p k n", p=128), [], [f"wbo{l}_{cb}h"], bg=True)
        for cb in range(8):
            DMA("pool", wb_out[l][cb], w_out[l][:, cb * 512:(cb + 1) * 512].rearrange("(k p) n -> p k n", p=128),
                [], [f"wbout{l}_{cb}"], bg=True)

    for l in range(DEPTH):
        cast_weights(l)
    S.barrier()

    tiles = []
    for i in range(TP // TT_):
        tiles.append(("p", i * TT_, 0))
    for i in range(LS // TT_):
        tiles.append(("s", i * TT_, 1))

    def x_rows(l, kind, off, n):
        if l == 0:
            return (xp if kind == "p" else xs)[off:off + n, :]
        g = off if kind == "p" else TP + off
        return xres[g:g + n, :]

    def y_rows(l, kind, off, n):
        if l == DEPTH - 1:
            return (y_p if kind == "p" else y_s)[off:off + n, :]
        g = off if kind == "p" else TP + off
        return xres[g:g + n, :]

    def tok0(kind, off):
        return off if kind == "p" else TP + off

    def stage_mod(l):
        A = Arena(PERSIST)
        modrow = A.f32(3 * D)
        bm2 = A.f32(3 * D)
        base2 = A.off
        wblk = [A.bf16(KC * 512).rearrange("p (k n) -> p k n", n=512) for _ in range(2)]
        for j in range(2):
            DMA("sp", bm2[j:j + 1, :], b_mod[l].rearrange("(o n) -> o n", o=1), [], [f"bm2_{j}"])
        DMA("sp", wblk[0], wb_mod[l][0], [f"wbmod{l}_0"], ["wblk0"])
        for cb in range(24):
            wb = wblk[cb % 2]
            if cb + 1 < 24:
                DMA("sp", wblk[(cb + 1) % 2], wb_mod[l][cb + 1], [f"wbmod{l}_{cb + 1}"], [f"wblk{(cb + 1) % 2}"])
            pb = cb % 2
            for k in range(KC):
                MM(psb(pb)[0:2, :], condT[:, k, :], wb[:, k, :], k == 0, k == KC - 1,
                   ["condT", f"wblk{cb % 2}"], [f"ps{pb}"])
            TTo("dve", modrow[0:2, cb * 512:(cb + 1) * 512], psb(pb)[0:2, :], bm2[0:2, cb * 512:(cb + 1) * 512],
                ALU.add, [f"ps{pb}", "bm2_0", "bm2_1"], ["modrow"])
        S.barrier()
        B = Arena(base2)
        gp2 = B.f32(D)
        gq2 = B.f32(D)
        s1row = B.f32(D)
        ggrow = B.f32(D)
        ggs = [B.f32(512) for _ in range(2)]
        for j in range(2):
            DMA("sp", gp2[j:j + 1, :], g_pre[l].rearrange("(o n) -> o n", o=1), [], [f"gp2_{j}"])
            DMA("sp", gq2[j:j + 1, :], g_post[l].rearrange("(o n) -> o n", o=1), [], [f"gq2_{j}"])
        STT(s1row[0:2, :], modrow[0:2, D:2 * D], 1.0, gp2[0:2, :], ALU.add, ALU.mult,
            ["modrow", "gp2_0", "gp2_1"], ["s1row"])
        TTo("dve", ggrow[0:2, :], modrow[0:2, 2 * D:3 * D], gq2[0:2, :], ALU.mult, ["modrow", "gq2_0", "gq2_1"],
            ["ggrow"])
        pmt = psb(2)[:, 0:128].rearrange("p (k j) -> p k j", j=2)
        for k in range(KC):
            TR(pmt[:, k, :], s1row[0:2, k * 128:(k + 1) * 128], ident_f[0:2, 0:2], ["s1row", "ident_f"], ["ps2"])
        for k in range(KC):
            TR(pmt[:, KC + k, :], modrow[0:2, k * 128:(k + 1) * 128], ident_f[0:2, 0:2], ["modrow", "ident_f"],
               ["ps2"])
        CP("dve", modT, pmt, ["ps2"], ["modT"])
        n = 0
        for j in range(2):
            for cb in range(8):
                pb = 4 + n % 2
                MM(psb(pb), sel_f[0:2, j * 128:(j + 1) * 128], ggrow[0:2, cb * 512:(cb + 1) * 512], True, True,
                   ["sel_f", "ggrow"], [f"ps{pb}"])
                CP("dve", ggs[n % 2], psb(pb), [f"ps{pb}"], [f"ggs{n % 2}"])
                DMA("sp", ggb[j][:, cb * 512:(cb + 1) * 512], ggs[n % 2], [f"ggs{n % 2}"], [f"ggb{j}"])
                n += 1
        S.barrier()

    def col_func(c0):
        if C_AG <= c0 < C_FI or C_FG <= c0 < C_HV or C_HG <= c0 < C_GA:
            return AF.Silu
        if c0 >= C_GA:
            return AF.Sigmoid
        return None

    def stage_proj(l, kind, off, j):
        A = Arena(PERSIST)
        hT = A.bf16(KC * 512).rearrange("p (k n) -> p k n", n=512)
        xn = A.bf16(4 * D).rearrange("p (s n) -> p s n", n=D)
        xf = [A.f32(D) for _ in range(2)]
        junk = A.bf16(D)
        ssq = A.f32(4)
        rstd = A.f32(4)
        wblk = [A.bf16(KC * 512).rearrange("p (k n) -> p k n", n=512) for _ in range(2)]
        ost = [A.bf16(512) for _ in range(4)]
        t0 = tok0(kind, off)
        DMA("sp", wblk[0], wb_in[l][0], [f"wbin{l}_0"], ["wblk0"])
        for s in range(4):
            xb = xf[s % 2]
            DMA("sp", xb, x_rows(l, kind, off + s * 128, 128), [], [f"xf{s % 2}"])
            ACT(junk, xb, AF.Square, [f"xf{s % 2}"], ["junk", f"ssq{s}"], accum=ssq[:, s:s + 1])
            RSQRT_ACT(rstd[:, s:s + 1], ssq[:, s:s + 1], 1.0 / D, [f"ssq{s}"], [f"rstd{s}"])
            TS("dve", xn[:, s, :], xb, rstd[:, s:s + 1], None, ALU.mult, None, [f"xf{s % 2}", f"rstd{s}"],
               [f"xn{s}"])
        for k in range(KC):
            pb = k % 2
            pv = psb16(pb)[:, 0:512]
            for s in range(4):
                TR(pv[:, s * 128:(s + 1) * 128], xn[:, s, k * 128:(k + 1) * 128], ident_bf, [f"xn{s}", "ident_bf"],
                   [f"ps{pb}"])
            ACT(hT[:, k, :], pv, AF.Identity, [f"ps{pb}", "modT"], [f"hT{k}"], bias=modT[:, KC + k, j:j + 1],
                scale=modT[:, k, j:j + 1])
        hkeys = [f"hT{k}" for k in range(KC)]
        n = 0
        for cb in range(46):
            wb = wblk[cb % 2]
            if cb + 1 < 46:
                DMA("sp", wblk[(cb + 1) % 2], wb_in[l][cb + 1], [f"wbin{l}_{cb + 1}"], [f"wblk{(cb + 1) % 2}"])
            for q in range(4):
                c0 = cb * 512 + q * 128
                pb = 2 + n % 6
                for k in range(KC):
                    MM(psb(pb), wb[:, k, q * 128:(q + 1) * 128], hT[:, k, :], k == 0, k == KC - 1,
                       [f"wblk{cb % 2}", f"hT{k}"], [f"ps{pb}"])
                o = ost[n % 4]
                fn = col_func(c0)
                if fn is None:
                    CP("dve", o, psb(pb), [f"ps{pb}"], [f"ost{n % 4}"])
                else:
                    ACT(o, psb(pb), fn, [f"ps{pb}"], [f"ost{n % 4}"])
                DMA("sp", projT[c0:c0 + 128, t0:t0 + 512], o, [f"ost{n % 4}"], [f"projT_{c0}_{t0}"])
                n += 1
        S.barrier()

    def stage_attn(l, t0, L, rope, seq_out, base=None, bar=True):
        A = Arena(PERSIST if base is None else base)
        Lk = L + (PAST if rope else 0)
        nkc = Lk // 128
        QT = min(512, L)
        kT = A.bf16(NKV * Lk).rearrange("p (h n) -> p h n", n=Lk)
        V = A.bf16(nkc * NKV * HD).rearrange("p (c h d) -> p c h d", h=NKV, d=HD)
        if rope:
            cosT = A.bf16(L)
            sinT = A.bf16(L)
            DMA("pool", cosT, tabs["rope_cos"], [], ["cosT"])
            DMA("pool", sinT, tabs["rope_sin"], [], ["sinT"])
            ckb = A.bf16(4 * 512).rearrange("p (c n) -> p c n", n=512)
            DMA("pool", ckb, ck[l].rearrange("(c p) n -> p c n", p=128), [], ["ckb"])
            DMA("pool", V[:, L // 128:nkc, :, :].rearrange("p c h d -> p c (h d)"),
                cv[l].rearrange("(c p) n -> p c n", p=128), [], ["Vctx"])
        raw = [A.bf16(QT) for _ in range(2)]
        sq = [A.bf16(QT) for _ in range(2)]
        rs = [A.f32(QT) for _ in range(2)]
        qn = [A.bf16(QT) for _ in range(2)]
        t1 = [A.f32(QT) for _ in range(2)]
        t2 = [A.f32(QT) for _ in range(2)]
        qT = [A.bf16(QT) for _ in range(2)]
        ag = [A.bf16(QT) for _ in range(2)]
        pT = [A.bf16(QT) for _ in range(3)]
        rd = [A.f32(QT) for _ in range(2)]
        ot = [A.f32(QT) for _ in range(2)]
        ob = [A.bf16(QT) for _ in range(2)]
        vraw = [A.bf16(L) for _ in range(2)]
        cst = [A.f32(512) for _ in range(2)]
        cnt = {"n": 0}

        def nr_load(src_rows, tq0, n):
            i = cnt["n"] % 2
            cnt["n"] += 1
            DMA("sp", raw[i][:, 0:n], projT[src_rows:src_rows + 128, t0 + tq0:t0 + tq0 + n], [], [f"raw{i}"])
            return i

        def norm_rope(src_rows, tq0, n, gcol, dst, dst_key, pre=None):
            i = nr_load(src_rows, tq0, n) if pre is None else pre
            R, SQ, RS, QN, T1, T2 = raw[i], sq[i], rs[i], qn[i], t1[i], t2[i]
            ACT(SQ[:, 0:n], R[:, 0:n], AF.Square, [f"raw{i}"], [f"sq{i}"])
            MM(psb(0)[:, 0:n], ones_bf, SQ[:, 0:n], True, True, ["ones_bf", f"sq{i}"], ["ps0"])
            RSQRT_ACT(RS[:, 0:n], psb(0)[:, 0:n], 1.0 / HD, ["ps0"], [f"rs{i}"])
            tgt = QN if rope else dst
            tkey = f"qn{i}" if rope else dst_key
            STT(tgt[:, 0:n] if rope else tgt, R[:, 0:n], gcol, RS[:, 0:n], ALU.mult, ALU.mult,
                [f"raw{i}", f"rs{i}"], [tkey])
            if rope:
                MM(psb(1)[:, 0:n], rm_bf, QN[:, 0:n], True, True, ["rm_bf", f"qn{i}"], ["ps1"])
                TTo("pool", T1[:, 0:n], QN[:, 0:n], cosT[:, tq0:tq0 + n], ALU.mult, [f"qn{i}", "cosT"], [f"t1{i}"])
                TTo("dve", T2[:, 0:n], psb(1)[:, 0:n], sinT[:, tq0:tq0 + n], ALU.mult, ["ps1", "sinT"], [f"t2{i}"])
                TTo("dve", dst, T1[:, 0:n], T2[:, 0:n], ALU.add, [f"t1{i}", f"t2{i}"], [dst_key])
            return QN if rope else dst

        ncq = 0
        import os as _os
        _cut = _os.environ.get("DBG_ATTN_CUT", "")
        for h in range(NKV):
            for tq in range(L // QT):
                norm_rope(C_K + h * 128, tq * QT, QT, qn_col[:, 2 * l + 1:2 * l + 2],
                          kT[:, h, tq * QT:(tq + 1) * QT], f"kT{h}")
            if _cut == "prepA":
                continue
            if seq_out is not None:
                for c in range(L // 128):
                    pv = psb16(2)[:, 0:128]
                    TR(pv, kT[:, h, c * 128:(c + 1) * 128], ident_bf, [f"kT{h}", "ident_bf"], ["ps2"])
                    cs = cst[ncq % 2]
                    CP("act", cs[:, 0:128], pv, ["ps2"], [f"cst{ncq % 2}"])
                    DMA("sp", nck[seq_out, l, c * 128:(c + 1) * 128, h * 128:(h + 1) * 128], cs[:, 0:128],
                        [f"cst{ncq % 2}"], [f"nck{h}_{c}"])
                    ncq += 1
            if rope:
                for c in range(4):
                    pv = psb16(2)[:, 0:128]
                    TR(pv, ckb[:, c, h * 128:(h + 1) * 128], ident_bf, ["ckb", "ident_bf"], ["ps2"])
                    CP("act", kT[:, h, L + c * 128:L + (c + 1) * 128], pv, ["ps2"], [f"kT{h}"])
            if _cut == "prepB":
                continue
            vr = vraw[h % 2]
            DMA("sp", vr, projT[C_V + h * 128:C_V + (h + 1) * 128, t0:t0 + L], [], [f"vraw{h % 2}"])
            for c in range(L // 128):
                pv = psb16(3)[:, 0:128]
                TR(pv, vr[:, c * 128:(c + 1) * 128], ident_bf, [f"vraw{h % 2}", "ident_bf"], ["ps3"])
                CP("act", V[:, c, h, :], pv, ["ps3"], [f"V{h}"])
                if seq_out is not None:
                    cs = cst[ncq % 2]
                    CP("dve", cs[:, 0:128], V[:, c, h, :], [f"V{h}"], [f"cst{ncq % 2}"])
                    DMA("sp", ncv[seq_out, l, c * 128:(c + 1) * 128, h * 128:(h + 1) * 128], cs[:, 0:128],
                        [f"cst{ncq % 2}"], [f"ncv{h}_{c}"])
                    ncq += 1
        it = 0
        npt = 0
        iters = [(h, g, tq) for h in range(NKV) for g in range(4) for tq in range(L // QT)]

        def pre_load(idx):
            h_, g_, tq_ = iters[idx]
            hd_ = h_ * 4 + g_
            ri = nr_load(C_Q + hd_ * 128, tq_ * QT, QT)
            DMA("sp", ag[idx % 2], projT[C_AG + hd_ * 128:C_AG + (hd_ + 1) * 128, t0 + tq_ * QT:t0 + (tq_ + 1) * QT],
                [], [f"ag{idx % 2}"])
            return ri

        pend = pre_load(0)
        for idx, (h, g, tq) in enumerate(iters):
            if True:
                hd = h * 4 + g
                if True:
                    i = it % 2
                    it += 1
                    cur = pend
                    if idx + 1 < len(iters):
                        pend = pre_load(idx + 1)
                    norm_rope(C_Q + hd * 128, tq * QT, QT, qn_col[:, 2 * l:2 * l + 1], qT[i], f"qT{i}", pre=cur)
                    po, pd = 4 + i, 6 + i
                    vkeys = [f"V{h}"] + (["Vctx"] if rope else [])
                    MM(psb(2)[:, 0:QT], kT[:, h, 0:128], qT[i], True, True, [f"kT{h}", f"qT{i}"], ["ps2"])
                    for c in range(nkc):
                        pss = 2 + c % 2
                        if c + 1 < nkc:
                            pn = 2 + (c + 1) % 2
                            MM(psb(pn)[:, 0:QT], kT[:, h, (c + 1) * 128:(c + 2) * 128], qT[i], True, True,
                               [f"kT{h}", f"qT{i}"], [f"ps{pn}"])
                        P = pT[npt % 3]
                        pk = f"pT{npt % 3}"
                        npt += 1
                        ACT(P, psb(pss)[:, 0:QT], AF.Exp, [f"ps{pss}"], [pk], scale=HD ** -0.5)
                        MM(psb(po)[:, 0:QT], V[:, c, h, :], P, c == 0, c == nkc - 1, vkeys + [pk], [f"ps{po}"])
                        MM(psb(pd)[:, 0:QT], ones_bf, P, c == 0, c == nkc - 1, ["ones_bf", pk], [f"ps{pd}"])
                    RECIP(rd[i], psb(pd)[:, 0:QT], [f"ps{pd}"], [f"rd{i}"])
                    TTo("dve", ot[i], psb(po)[:, 0:QT], rd[i], ALU.mult, [f"ps{po}", f"rd{i}"], [f"ot{i}"])
                    TTo("dve" if _cut == "nopool" else "pool", ob[i], ot[i], ag[i], ALU.mult, [f"ot{i}", f"ag{i}"], [f"ob{i}"])
                    DMA("sp", bT[hd * 128:(hd + 1) * 128, t0 + tq * QT:t0 + (tq + 1) * QT], ob[i], [f"ob{i}"],
                        [f"bT_a{hd}_{tq}"])
        assert A.off <= (ARENA_W if base is None else base + SLOT), A.off
        if bar:
            S.barrier()

    def stage_fnet(l, t0, L, base=None, bar=True):
        A = Arena(PERSIST if base is None else base)
        QT = min(512, L)
        nt = L // 128
        UT = A.bf16(8 * L).rearrange("p (c n) -> p c n", n=L)
        cw = A.bf16(512).rearrange("p (c n) -> p c n", n=256)
        sw = A.bf16(512).rearrange("p (c n) -> p c n", n=256)
        Asb = A.bf16(nt * 1024).rearrange("p (t n) -> p t n", n=1024)
        Bsb = A.bf16(nt * 1024).rearrange("p (t n) -> p t n", n=1024)
        clt = [A.bf16(nt * QT).rearrange("p (t n) -> p t n", n=QT) for _ in range(2)]
        slt = [A.bf16(nt * QT).rearrange("p (t n) -> p t n", n=QT) for _ in range(2)]
        fg = [A.bf16(QT) for _ in range(2)]
        ost = [A.bf16(QT) for _ in range(2)]
        DMA("sp", UT, projT[C_FI:C_FI + 1024, t0:t0 + L].rearrange("(c p) t -> p c t", p=128), [], ["UT"])
        DMA("pool", cw, tabs["fn_cw"].rearrange("(c p) n -> p c n", p=128), [], ["cw"])
        DMA("pool", sw, tabs["fn_sw"].rearrange("(c p) n -> p c n", p=128), [], ["sw"])
        for tt in range(nt):
            pa, pb_ = (0, 2) if tt % 2 == 0 else (4, 6)
            for g in range(4):
                bank, o = divmod(g * 256, 512)
                for hf in range(2):
                    MM(psb(pa + bank)[:, o:o + 256], UT[:, g * 2 + hf, tt * 128:(tt + 1) * 128], cw[:, hf, :],
                       hf == 0, hf == 1, ["UT", "cw"], [f"ps{pa + bank}"])
                for hf in range(2):
                    MM(psb(pb_ + bank)[:, o:o + 256], UT[:, g * 2 + hf, tt * 128:(tt + 1) * 128], sw[:, hf, :],
                       hf == 0, hf == 1, ["UT", "sw"], [f"ps{pb_ + bank}"])
            for bank in range(2):
                CP("act", Asb[:, tt, bank * 512:(bank + 1) * 512], psb(pa + bank), [f"ps{pa + bank}"], [f"Asb{tt}"])
                CP("dve", Bsb[:, tt, bank * 512:(bank + 1) * 512], psb(pb_ + bank), [f"ps{pb_ + bank}"],
                   [f"Bsb{tt}"])
        akeys = [f"Asb{tt}" for tt in range(nt)]
        bkeys = [f"Bsb{tt}" for tt in range(nt)]
        n = 0
        for tq in range(L // QT):
            i = tq % 2
            DMA("pool", clt[i], tabs[f"fn_cl{L}"][:, tq * QT:(tq + 1) * QT].rearrange("(t p) n -> p t n", p=128), [],
                [f"clt{i}"])
            DMA("pool", slt[i], tabs[f"fn_nsl{L}"][:, tq * QT:(tq + 1) * QT].rearrange("(t p) n -> p t n", p=128),
                [], [f"slt{i}"])
            for c in range(8):
                pb = n % 4
                for tch in range(nt):
                    MM(psb(pb)[:, 0:QT], Asb[:, tch, c * 128:(c + 1) * 128], clt[i][:, tch, :], tch == 0, False,
                       [f"Asb{tch}", f"clt{i}"], [f"ps{pb}"])
                for tch in range(nt):
                    MM(psb(pb)[:, 0:QT], Bsb[:, tch, c * 128:(c + 1) * 128], slt[i][:, tch, :], False,
                       tch == nt - 1, [f"Bsb{tch}", f"slt{i}"], [f"ps{pb}"])
                o = ost[n % 2]
                DMA("sp", fg[n % 2], projT[C_FG + c * 128:C_FG + (c + 1) * 128, t0 + tq * QT:t0 + (tq + 1) * QT], [],
                    [f"fg{n % 2}"])
                TTo("dve", o, psb(pb)[:, 0:QT], fg[n % 2], ALU.mult, [f"ps{pb}", f"fg{n % 2}"], [f"ost{n % 2}"])
                DMA("sp", bT[2048 + c * 128:2048 + (c + 1) * 128, t0 + tq * QT:t0 + (tq + 1) * QT], o,
                    [f"ost{n % 2}"], [f"bT_f{c}_{tq}"])
                n += 1
        assert A.off <= (ARENA_W if base is None else base + SLOT), A.off
        if bar:
            S.barrier()

    def stage_filters(l, L):
        A = Arena(PERSIST)
        nt = L // 128
        CH = min(512, L)
        featT = A.f32(L)
        w1 = A.f32(64)
        w2 = A.f32(64)
        w3a = A.f32(4096)
        colp = A.f32(8)
        h1T = A.f32(L)
        h2a = A.f32(L)
        tmp = [A.f32(CH) for _ in range(2)]
        tmq = [A.f32(CH) for _ in range(2)]
        MAGIC = 12582912.0
        delta = A.f32(1024)
        tneg = A.f32(nt)
        dec = [A.f32(512) for _ in range(2)]
        hf = [A.f32(512) for _ in range(2)]
        hb = [A.f32(512) for _ in range(2)]
        ab = [A.f32(512) for _ in range(2)]
        ab2 = [A.f32(512) for _ in range(2)]
        hs = A.bf16(nt * 512).rearrange("p (t n) -> p t n", n=512)
        hdd = A.bf16(nt * 512).rearrange("p (t n) -> p t n", n=512)
        rn = A.f32(512)
        cft = [A.bf16(nt * 128).rearrange("p (t n) -> p t n", n=128) for _ in range(2)]
        sft = [A.bf16(nt * 128).rearrange("p (t n) -> p t n", n=128) for _ in range(2)]
        gst = [A.bf16(512) for _ in range(4)]
        DMA("sp", featT[0:33, :], tabs[f"hy_feat{L}"], [], ["featT"])
        DMA("sp", w1[0:33, :], hy_w1[l], [], ["w1"])
        DMA("sp", w2[0:64, :], hy_w2[l], [], ["w2"])
        DMA("sp", w3a[0:64, :], hy_w3[l], [], ["w3a"])
        DMA("sp", w3a[64:65, :], hy_b3[l].rearrange("(o n) -> o n", o=1), [], ["w3b"])
        DMA("sp", colp[0:64, 0:1], hy_fr[l].rearrange("(d o) -> d o", o=1), [], ["colp0"], slow=True)
        DMA("sp", colp[0:64, 1:2], hy_b1[l].rearrange("(d o) -> d o", o=1), [], ["colp1"], slow=True)
        DMA("sp", colp[0:64, 2:3], hy_b2[l].rearrange("(d o) -> d o", o=1), [], ["colp2"], slow=True)
        DMA("sp", delta, tabs["hy_delta"], [], ["delta"])
        DMA("sp", tneg, tabs[f"hy_tneg{L}"], [], ["tneg"])
        OFF = 17.0 * math.pi
        for q in (1, 2):
            TTo("dve", colp[0:64, 2 + q:3 + q], colp[0:64, q:q + 1], colp[0:64, 0:1], ALU.mult,
                ["colp0", f"colp{q}"], [f"colp{2 + q}"])
        MSET("dve", h2a[64:65, :], 1.0, [], ["h2ones"])
        for layer, (wt, kdim, src, dst, bcol, wk, sk, dk) in enumerate((
                (w1, 33, featT, h1T, 3, "w1", "featT", "h1T"), (w2, 64, h1T, h2a, 4, "w2", "h1T", "h2T"))):
            for c in range(L // CH):
                i = c % 2
                MM(psb(i)[0:64, 0:CH], wt[0:kdim, :], src[0:kdim, c * CH:(c + 1) * CH], True, True, [wk, sk],
                   [f"ps{i}"])
                TS("dve", tmp[i][0:64, :], psb(i)[0:64, 0:CH], colp[0:64, 0:1], colp[0:64, bcol:bcol + 1], ALU.mult,
                   ALU.add, [f"ps{i}", "colp0", f"colp{bcol}"], [f"tmp{i}"])
                TS("dve", tmq[i][0:64, :], tmp[i][0:64, :], 1.0 / TWO_PI, MAGIC, ALU.mult, ALU.add, [f"tmp{i}"],
                   [f"tmq{i}"])
                TS("dve", tmq[i][0:64, :], tmq[i][0:64, :], -MAGIC, None, ALU.add, None, [f"tmq{i}"], [f"tmq{i}"])
                STT(tmp[i][0:64, :], tmq[i][0:64, :], -TWO_PI, tmp[i][0:64, :], ALU.mult, ALU.add,
                    [f"tmq{i}", f"tmp{i}"], [f"tmp{i}"])
                ACT(dst[0:64, c * CH:(c + 1) * CH], tmp[i][0:64, :], AF.Sin, [f"tmp{i}"], [dk])
        n = 0
        ng = 0
        for o in range(2):
            for chf in range(2):
                cF = o * 2048 + chf * 512
                cB = o * 2048 + 1024 + chf * 512
                for pt in range(nt):
                    i = n % 2
                    n += 1
                    MM(psb(i), h2a[0:65, pt * 128:(pt + 1) * 128], w3a[0:65, cF:cF + 512], True, True,
                       ["h2T", "h2ones", "w3a", "w3b"], [f"ps{i}"])
                    MM(psb(2 + i), h2a[0:65, pt * 128:(pt + 1) * 128], w3a[0:65, cB:cB + 512], True, True,
                       ["h2T", "h2ones", "w3a", "w3b"], [f"ps{2 + i}"])
                    ACT(dec[i], delta[:, chf * 512:(chf + 1) * 512], AF.Exp, ["delta", "tneg"], [f"dec{i}"],
                        scale=tneg[:, pt:pt + 1])
                    TTo("dve", hf[i], psb(i), dec[i], ALU.mult, [f"ps{i}", f"dec{i}"], [f"hf{i}"])
                    TTo("dve", hb[i], psb(2 + i), dec[i], ALU.mult, [f"ps{2 + i}", f"dec{i}"], [f"hb{i}"])
                    STT(ab[i], hf[i], -1.0, hf[i], ALU.mult, ALU.max, [f"hf{i}"], [f"ab{i}"])
                    STT(ab2[i], hb[i], -1.0, hb[i], ALU.mult, ALU.max, [f"hb{i}"], [f"ab2{i}"])
                    TTo("pool", ab[i], ab[i], ab2[i], ALU.add, [f"ab{i}", f"ab2{i}"], [f"ab{i}"])
                    MM(psb(4), ones_f, ab[i], pt == 0, pt == nt - 1, ["ones_f", f"ab{i}"], ["ps4"])
                    TTo("pool", hs[:, pt, :], hf[i], hb[i], ALU.add, [f"hf{i}", f"hb{i}"], [f"hs{pt}"])
                    TTo("pool", hdd[:, pt, :], hf[i], hb[i], ALU.subtract, [f"hf{i}", f"hb{i}"], [f"hd{pt}"])
                TS("dve", rn, psb(4), EPS, None, ALU.add, None, ["ps4"], ["rn"])
                RECIP(rn, rn, ["rn"], ["rn"])
                for kt in range(nt):
                    i = kt % 2
                    DMA("pool", cft[i], tabs[f"hy_cf{L}"][:, kt * 128:(kt + 1) * 128].rearrange(
                        "(t p) n -> p t n", p=128), [], [f"cft{i}"])
                    DMA("pool", sft[i], tabs[f"hy_sf{L}"][:, kt * 128:(kt + 1) * 128].rearrange(
                        "(t p) n -> p t n", p=128), [], [f"sft{i}"])
                    pr, pi_ = 5 + 0, 6 + i % 2
                    pr = 5 if i == 0 else 7
                    pi_ = 6 if i == 0 else 0
                    for sch in range(nt):
                        MM(psb(pr), cft[i][:, sch, :], hs[:, sch, :], sch == 0, sch == nt - 1,
                           [f"cft{i}", f"hs{sch}"], [f"ps{pr}"])
                    for sch in range(nt):
                        MM(psb(pi_), sft[i][:, sch, :], hdd[:, sch, :], sch == 0, sch == nt - 1,
                           [f"sft{i}", f"hd{sch}"], [f"ps{pi_}"])
                    for ri, pbank in ((0, pr), (1, pi_)):
                        gs = gst[ng % 4]
                        TTo("dve", gs, psb(pbank), rn, ALU.mult, [f"ps{pbank}", "rn"], [f"gst{ng % 4}"])
                        DMA("sp", gspec[L][o, ri, kt * 128:(kt + 1) * 128, chf * 512:(chf + 1) * 512], gs,
                            [f"gst{ng % 4}"], [f"gspec{o}_{ri}_{kt}_{chf}"])
                        ng += 1
        S.barrier()

    def stage_hyena(l, t0, L, chf, base=None, bar=True):
        A = Arena(PERSIST if base is None else base)
        nt = L // 128
        QT = min(256, L)
        scw = A.f32(36)
        scb = A.f32(12)
        hbias = A.f32(8)
        vT = A.bf16(4 * L).rearrange("p (c n) -> p c n", n=L)
        xm = A.bf16(4 * L).rearrange("p (c n) -> p c n", n=L)
        z1 = A.bf16(4 * L).rearrange("p (c n) -> p c n", n=L)
        ztm = A.bf16(nt * 512).rearrange("p (t n) -> p t n", n=512)
        Yr = A.bf16(nt * 512).rearrange("p (t n) -> p t n", n=512)
        Yi = A.bf16(nt * 512).rearrange("p (t n) -> p t n", n=512)
        rawc = [A.bf16(L) for _ in range(2)]
        acc = [A.f32(L) for _ in range(1)]
        cft = [A.bf16(nt * 128).rearrange("p (t n) -> p t n", n=128) for _ in range(2)]
        sft = [A.bf16(nt * 128).rearrange("p (t n) -> p t n", n=128) for _ in range(2)]
        gr = [A.bf16(512) for _ in range(2)]
        gi = [A.bf16(512) for _ in range(2)]
        pa = [A.f32(512) for _ in range(2)]
        cit = A.bf16(nt * QT).rearrange("p (t n) -> p t n", n=QT)
        sit = A.bf16(nt * QT).rearrange("p (t n) -> p t n", n=QT)
        yt = [A.f32(QT) for _ in range(2)]
        hg = [A.bf16(QT) for _ in range(2)]
        z2 = [A.bf16(QT) for _ in range(2)]
        ost = [A.bf16(QT) for _ in range(2)]
        c0 = chf * 512
        for part in range(3):
            for tap in range(3):
                DMA("sp", scw[:, part * 12:(part + 1) * 12].rearrange("p (c t) -> p c t", t=3)[:, :, tap],
                    hy_sw[l, tap, part * 1024 + c0:part * 1024 + c0 + 512].rearrange("(c p) -> p c", p=128), [],
                    ["scw"], slow=True)
            DMA("sp", scb[:, part * 4:(part + 1) * 4],
                hy_sb[l, part * 1024 + c0:part * 1024 + c0 + 512].rearrange("(c p) -> p c", p=128), [], ["scb"],
                slow=True)
        for o in range(2):
            DMA("sp", hbias[:, o * 4:(o + 1) * 4], hy_bias[l, o, c0:c0 + 512].rearrange("(c p) -> p c", p=128), [],
                ["hbias"], slow=True)
        nr = {"n": 0}

        def short_conv(part, dst, dkey):
            base = (C_HV, C_HX1, C_HX2)[part] + c0
            for c in range(4):
                i = nr["n"] % 2
                nr["n"] += 1
                R, AC = rawc[i], acc[0]
                DMA("sp", R, projT[base + c * 128:base + (c + 1) * 128, t0:t0 + L], [], [f"rawc{i}"])
                wv = scw[:, part * 12 + c * 3:part * 12 + c * 3 + 3]
                TS("dve", AC, R, wv[:, 1:2], scb[:, part * 4 + c:part * 4 + c + 1], ALU.mult, ALU.add,
                   [f"rawc{i}", "scw", "scb"], ["acc0"])
                STT(AC[:, 1:L], R[:, 0:L - 1], wv[:, 0:1], AC[:, 1:L], ALU.mult, ALU.add, [f"rawc{i}", "scw", "acc0"],
                    ["acc0"])
                STT(AC[:, 0:L - 1], R[:, 1:L], wv[:, 2:3], AC[:, 0:L - 1], ALU.mult, ALU.add,
                    [f"rawc{i}", "scw", "acc0"], ["acc0"])
                CP("pool", dst[:, c, :], AC, ["acc0"], [f"{dkey}{c}"])

        def conv(o, zin, zkey, last):
            for tch in range(nt):
                pb = tch % 2
                pv = psb16(pb)[:, 0:512]
                for c in range(4):
                    TR(pv[:, c * 128:(c + 1) * 128], zin[:, c, tch * 128:(tch + 1) * 128], ident_bf,
                       [f"{zkey}{c}", "ident_bf"], [f"ps{pb}"])
                CP("act", ztm[:, tch, :], pv, [f"ps{pb}"], [f"ztm{tch}"])
            def fwd_loads(kt_):
                i_ = kt_ % 2
                DMA("pool", cft[i_], tabs[f"hy_cf{L}"][:, kt_ * 128:(kt_ + 1) * 128].rearrange(
                    "(t p) n -> p t n", p=128), [], [f"cft{i_}"])
                DMA("pool", sft[i_], tabs[f"hy_sf{L}"][:, kt_ * 128:(kt_ + 1) * 128].rearrange(
                    "(t p) n -> p t n", p=128), [], [f"sft{i_}"])
                DMA("sp", gr[i_], gspec[L][o, 0, kt_ * 128:(kt_ + 1) * 128, c0:c0 + 512], [], [f"gr{i_}"])
                DMA("sp", gi[i_], gspec[L][o, 1, kt_ * 128:(kt_ + 1) * 128, c0:c0 + 512], [], [f"gi{i_}"])

            fwd_loads(0)
            for kt in range(nt):
                i = kt % 2
                if kt + 1 < nt:
                    fwd_loads(kt + 1)
                pr, pi_ = 2 + 2 * i, 3 + 2 * i
                for sch in range(nt):
                    MM(psb(pr), cft[i][:, sch, :], ztm[:, sch, :], sch == 0, sch == nt - 1,
                       [f"cft{i}", f"ztm{sch}"], [f"ps{pr}"])
                for sch in range(nt):
                    MM(psb(pi_), sft[i][:, sch, :], ztm[:, sch, :], sch == 0, sch == nt - 1,
                       [f"sft{i}", f"ztm{sch}"], [f"ps{pi_}"])
                TTo("dve", pa[0], psb(pr), gr[i], ALU.mult, [f"ps{pr}", f"gr{i}"], ["pa0"])
                TTo("dve", pa[1], psb(pi_), gi[i], ALU.mult, [f"ps{pi_}", f"gi{i}"], ["pa1"])
                TTo("pool", Yr[:, kt, :], pa[0], pa[1], ALU.subtract, ["pa0", "pa1"], [f"Yr{kt}"])
                TTo("dve", pa[0], psb(pr), gi[i], ALU.mult, [f"ps{pr}", f"gi{i}"], ["pa0"])
                TTo("dve", pa[1], psb(pi_), gr[i], ALU.mult, [f"ps{pi_}", f"gr{i}"], ["pa1"])
                TTo("pool", Yi[:, kt, :], pa[0], pa[1], ALU.add, ["pa0", "pa1"], [f"Yi{kt}"])
            n = 0
            for tq in range(L // QT):
                DMA("pool", cit, tabs[f"hy_cft{L}"][:, tq * QT:(tq + 1) * QT].rearrange("(t p) n -> p t n", p=128),
                    [], ["cit"])
                DMA("pool", sit, tabs[f"hy_sft{L}"][:, tq * QT:(tq + 1) * QT].rearrange("(t p) n -> p t n", p=128),
                    [], ["sit"])
                for c in range(4):
                    pb = 6 + n % 2
                    i = n % 2
                    n += 1
                    for kch in range(nt):
                        MM(psb(pb)[:, 0:QT], Yr[:, kch, c * 128:(c + 1) * 128], cit[:, kch, :], kch == 0, False,
                           [f"Yr{kch}", "cit"], [f"ps{pb}"])
                    for kch in range(nt):
                        MM(psb(pb)[:, 0:QT], Yi[:, kch, c * 128:(c + 1) * 128], sit[:, kch, :], False, kch == nt - 1,
                           [f"Yi{kch}", "sit"], [f"ps{pb}"])
                    ts_ = slice(tq * QT, (tq + 1) * QT)
                    STT(yt[i], zin[:, c, ts_], hbias[:, o * 4 + c:o * 4 + c + 1], psb(pb)[:, 0:QT], ALU.mult, ALU.add,
                        [f"{zkey}{c}", "hbias", f"ps{pb}"], [f"yt{i}"])
                    if not last:
                        TTo("dve", z1[:, c, ts_], yt[i], xm[:, c, ts_], ALU.mult, [f"yt{i}", f"xm{c}"], [f"z1{c}"])
                    else:
                        gbase = C_HG + c0 + c * 128
                        DMA("sp", hg[i], projT[gbase:gbase + 128, t0 + tq * QT:t0 + (tq + 1) * QT], [], [f"hg{i}"])
                        TTo("dve", z2[i], yt[i], xm[:, c, ts_], ALU.mult, [f"yt{i}", f"xm{c}"], [f"z2{i}"])
                        TTo("pool", ost[i], z2[i], hg[i], ALU.mult, [f"z2{i}", f"hg{i}"], [f"ost{i}"])
                        r0 = 3072 + c0 + c * 128
                        DMA("sp", bT[r0:r0 + 128, t0 + tq * QT:t0 + (tq + 1) * QT], ost[i], [f"ost{i}"],
                            [f"bT_h{c}_{tq}"])

        short_conv(0, vT, "vT")
        short_conv(1, xm, "xm")
        conv(0, vT, "vT", False)
        short_conv(2, xm, "xm")
        conv(1, z1, "z1", True)
        assert A.off <= (ARENA_W if base is None else base + SLOT), A.off
        if bar:
            S.barrier()

    def stage_out(l, kind, off, j):
        t0 = tok0(kind, off)
        A = Arena(PERSIST)
        bTt = A.bf16(KC * 512).rearrange("p (k n) -> p k n", n=512)
        mT = A.bf16(KC * 512).rearrange("p (k n) -> p k n", n=512)
        base2 = A.off
        wblk = [A.bf16(KC * 512).rearrange("p (k n) -> p k n", n=512) for _ in range(2)]
        sg = [[A.bf16(512) for _ in range(3)] for _ in range(2)]
        ta = [A.f32(512) for _ in range(2)]
        tb = [A.f32(512) for _ in range(2)]
        tc_ = [A.f32(512) for _ in range(2)]
        DMA("sp", wblk[0], wb_o[l][0], [f"wbo{l}_0a", f"wbo{l}_0f", f"wbo{l}_0h"], ["wblk0"])
        DMA("sp", bTt, bT[:, t0:t0 + 512].rearrange("(k p) t -> p k t", p=128), [], ["bTt"])

        def load_sg(nn):
            cq = nn * 128
            for bi, cg in enumerate((C_GA, C_GF, C_GH)):
                DMA("sp", sg[nn % 2][bi], projT[cg + cq:cg + cq + 128, t0:t0 + 512], [], [f"sg{nn % 2}_{bi}"])

        load_sg(0)
        n = 0
        for cb in range(8):
            wb = wblk[cb % 2]
            if cb + 1 < 8:
                DMA("sp", wblk[(cb + 1) % 2], wb_o[l][cb + 1],
                    [f"wbo{l}_{cb + 1}a", f"wbo{l}_{cb + 1}f", f"wbo{l}_{cb + 1}h"], [f"wblk{(cb + 1) % 2}"])
            for q in range(4):
                c0 = cb * 512 + q * 128
                i = n % 2
                n += 1
                if n < 32:
                    load_sg(n)
                pbs = (2 + 3 * i, 3 + 3 * i, 4 + 3 * i)
                for bi, (k0, k1) in enumerate(((0, 16), (16, 24), (24, 32))):
                    for k in range(k0, k1):
                        MM(psb(pbs[bi]), wb[:, k, q * 128:(q + 1) * 128], bTt[:, k, :], k == k0, k == k1 - 1,
                           [f"wblk{cb % 2}", "bTt"], [f"ps{pbs[bi]}"])
                TTo("dve", ta[i], psb(pbs[0]), sg[i][0], ALU.mult, [f"ps{pbs[0]}", f"sg{i}_0"], [f"ta{i}"])
                TTo("dve", tb[i], psb(pbs[1]), sg[i][1], ALU.mult, [f"ps{pbs[1]}", f"sg{i}_1"], [f"tb{i}"])
                TTo("dve", tc_[i], psb(pbs[2]), sg[i][2], ALU.mult, [f"ps{pbs[2]}", f"sg{i}_2"], [f"tc{i}"])
                TTo("pool", ta[i], ta[i], tb[i], ALU.add, [f"ta{i}", f"tb{i}"], [f"ta{i}"])
                TTo("pool", mT[:, cb * 4 + q, :], ta[i], tc_[i], ALU.add, [f"ta{i}", f"tc{i}"], [f"mT{cb * 4 + q}"])
        S.barrier()
        oT = bTt
        n = 0
        DMA("sp", wblk[0], wb_out[l][0], [f"wbout{l}_0"], ["wblk0"])
        for cb in range(8):
            wb = wblk[cb % 2]
            if cb + 1 < 8:
                DMA("sp", wblk[(cb + 1) % 2], wb_out[l][cb + 1], [f"wbout{l}_{cb + 1}"], [f"wblk{(cb + 1) % 2}"])
            for q in range(4):
                pb = 2 + n % 6
                n += 1
                for k in range(KC):
                    MM(psb(pb), wb[:, k, q * 128:(q + 1) * 128], mT[:, k, :], k == 0, k == KC - 1,
                       [f"wblk{cb % 2}", f"mT{k}"], [f"ps{pb}"])
                CP("act" if n % 2 else "dve", oT[:, cb * 4 + q, :], psb(pb), [f"ps{pb}"], [f"oT{cb * 4 + q}"])
        S.barrier()
        B = Arena(base2)
        xf = [B.f32(D) for _ in range(2)]
        tf = [B.f32(D) for _ in range(2)]
        gg = B.f32(D)
        junk = B.bf16(D)
        otm = B.bf16(D)
        ssq = B.f32(4)
        rstd = B.f32(4)
        DMA("sp", gg, ggb[j], [], ["gg"])
        for s in range(4):
            i = s % 2
            DMA("sp", xf[i], x_rows(l, kind, off + s * 128, 128), [], [f"xf{i}"])
            pv = ps[:, 4 * i:4 * i + 4, :].rearrange("p b n -> p (b n)").bitcast(BF16)[:, 0:D]
            pkeys = [f"ps{4 * i + b}" for b in range(4)]
            for k in range(KC):
                TR(pv[:, k * 128:(k + 1) * 128], oT[:, k, s * 128:(s + 1) * 128], ident_bf, [f"oT{k}", "ident_bf"],
                   pkeys)
            ACT(junk, pv, AF.Square, pkeys, ["junk", f"ssq{s}"], accum=ssq[:, s:s + 1])
            RSQRT_ACT(rstd[:, s:s + 1], ssq[:, s:s + 1], 1.0 / D, [f"ssq{s}"], [f"rstd{s}"])
            CP("act", otm, pv, pkeys, ["otm"])
            STT(tf[i], otm, rstd[:, s:s + 1], gg, ALU.mult, ALU.mult, ["otm", f"rstd{s}", "gg"], [f"tf{i}"])
            TTo("dve", tf[i], tf[i], xf[i], ALU.add, [f"tf{i}", f"xf{i}"], [f"tf{i}"])
            DMA("sp", y_rows(l, kind, off + s * 128, 128), tf[i], [f"tf{i}"], [f"y_{kind}_{off}_{s}"])
        S.barrier()

    stages = getattr(cfg, "stages", None)

    def on(name):
        return stages is None or name in stages

    for l in range(DEPTH):
        if on("mod"):
            stage_mod(l)
        if on("proj"):
            for (kind, off, j) in tiles:
                stage_proj(l, kind, off, j)
        if on("filt"):
            for L in Ls:
                stage_filters(l, L)
        NOBAR = False
        for s in range(NPS):
            if on("attn_p"):
                stage_attn(l, s * LP, LP, False, s, base=PERSIST, bar=not NOBAR)
            if on("fnet_p"):
                stage_fnet(l, s * LP, LP, base=PERSIST + SLOT, bar=not NOBAR)
            if on("hy_p"):
                for chf in range(2):
                    stage_hyena(l, s * LP, LP, chf, base=PERSIST + 2 * SLOT, bar=not NOBAR)
        S.barrier()
        if on("attn_s"):
            stage_attn(l, TP, LS, True, None)
        if on("fnet_s"):
            stage_fnet(l, TP, LS)
        if on("hy_s"):
            for chf in range(2):
                stage_hyena(l, TP, LS, chf)
        if cfg.debug and l == 0:
            for r0 in range(0, IN_W, 512):
                DMA("sp", dbg_proj[r0:r0 + 512, :], projT[r0:r0 + 512, :], [], [f"dbg_proj{r0}"])
            for r0 in range(0, D, 512):
                DMA("sp", dbg_bT[r0:r0 + 512, :], bT[r0:r0 + 512, :], [], [f"dbg_bT{r0}"])
            S.barrier()
        if on("out"):
            for (kind, off, j) in tiles:
                stage_out(l, kind, off, j)

    sem_cms = [nc.semaphore(f"e_{e}") for e in Sched.ENGS] + [nc.semaphore(f"d_{i}") for i in range(Sched.NDMA)]
    sems = [c.__enter__() for c in sem_cms]
    eng_sems = {e: sems[i] for i, e in enumerate(Sched.ENGS)}
    dma_sems = sems[len(Sched.ENGS):]
    with nc.Block() as block:
        S.emit(nc, block, eng_sems, dma_sems)
    nops = {e: len(S.ops[e]) for e in Sched.ENGS}
    return nc, nops


NCORES = 4


def kernel(x_prompt, x_sample, cache_k, cache_v, c, c_ctx, w_mod, b_mod, g_pre, w_in, q_norm, k_norm,
           hy_short_w, hy_short_b, hy_ffn_w1, hy_ffn_b1, hy_ffn_w2, hy_ffn_b2, hy_ffn_w3, hy_ffn_b3,
           hy_sin_freq, hy_bias, w_attn_o, w_fnet_o, w_hy_o, w_out, g_post):
    f = lambda a: np.ascontiguousarray(np.asarray(a), dtype=np.float32)
    x_prompt, x_sample, cache_k, cache_v, c, c_ctx = map(f, (x_prompt, x_sample, cache_k, cache_v, c, c_ctx))
    B, SEQ, _ = x_prompt.shape
    DB, LS, _ = x_sample.shape
    depth = w_mod.shape[0]
    nps = B // NCORES
    cfg = Cfg(ncores=NCORES, depth=depth, nps=nps, ls=LS)
    nc, _ = build_program(cfg)
    tabs = make_tables(LS)
    shared = {
        "w_mod": f(w_mod), "b_mod": f(b_mod), "g_pre": f(g_pre), "w_in": f(w_in), "q_norm": f(q_norm),
        "k_norm": f(k_norm), "hy_short_w": f(hy_short_w), "hy_short_b": f(hy_short_b), "hy_ffn_w1": f(hy_ffn_w1),
        "hy_ffn_b1": f(hy_ffn_b1), "hy_ffn_w2": f(hy_ffn_w2), "hy_ffn_b2": f(hy_ffn_b2), "hy_ffn_w3": f(hy_ffn_w3),
        "hy_ffn_b3": f(hy_ffn_b3), "hy_sin_freq": f(hy_sin_freq), "hy_bias": f(hy_bias), "w_attn_o": f(w_attn_o),
        "w_fnet_o": f(w_fnet_o), "w_hy_o": f(w_hy_o), "w_out": f(w_out), "g_post": f(g_post),
    }
    shared.update(tabs)
    in_maps = []
    for core in range(NCORES):
        b = core * DB // NCORES
        m = dict(shared)
        m["xp"] = x_prompt[core * nps:(core + 1) * nps].reshape(nps * SEQ, D)
        m["xs"] = x_sample[b]
        m["ck"] = cache_k[b].reshape(depth, PAST, NKV * HD)
        m["cv"] = cache_v[b].reshape(depth, PAST, NKV * HD)
        m["cond"] = np.stack([c_ctx, c[b]], axis=0)
        in_maps.append(m)
    res = run_bass_kernel_spmd(nc, in_maps, core_ids=list(range(NCORES)))
    r = res.results
    y_prompt = np.concatenate([r[k]["y_p"].reshape(nps, SEQ, D) for k in range(NCORES)], axis=0)
    y_sample = np.stack([r[(bb * NCORES) // DB]["y_s"] for bb in range(DB)], axis=0)
    nk = np.concatenate([r[k]["nck"].reshape(nps, depth, SEQ, NKV, HD) for k in range(NCORES)], axis=0)
    nv = np.concatenate([r[k]["ncv"].reshape(nps, depth, SEQ, NKV, HD) for k in range(NCORES)], axis=0)
    return (y_prompt.astype(np.float32), y_sample.astype(np.float32), nk.astype(np.float32), nv.astype(np.float32))
```

```python
import math
import numpy as np
import concourse.bass as bass
import concourse.mybir as mybir
from concourse.bass_utils import run_bass_kernel_spmd

F32 = mybir.dt.float32
BF16 = mybir.dt.bfloat16
AF = mybir.ActivationFunctionType
ALU = mybir.AluOpType

D = 4096
KC = 32
HD = 128
NH = 16
NKV = 4
LP = 256
PAST = 512
IN_W = 23552
EPS = 1e-6
TT_ = 512
C_Q, C_K, C_V, C_AG, C_FI, C_FG, C_HV, C_HX1, C_HX2, C_HG, C_GA, C_GF, C_GH = (
    0, 2048, 2560, 3072, 5120, 6144, 7168, 8192, 9216, 10240, 11264, 15360, 19456)
TWO_PI = 2.0 * math.pi
HY_MIN_DECAY = math.log(1e-2) / 0.3
HY_MAX_DECAY = math.log(1e-2) / 1.5


class _Op:
    __slots__ = ("eng", "fn", "deps", "sig", "dma", "semi", "cnt", "seq")


class Sched:
    ENGS = ("pe", "act", "dve", "pool", "sp")
    NREG = 56
    NBG = 8
    NDMA = NREG + NBG

    def __init__(self):
        self.ops = {e: [] for e in self.ENGS}
        self.last_w = {}
        self.readers = {}
        self.dma_since_barrier = []
        self.last_op = {e: None for e in self.ENGS}
        self.bar_dep = {e: None for e in self.ENGS}
        self.dma_n = 0
        self.bg_n = 0
        self.bg_w = {}
        self.dma_last_on_sem = [None] * self.NDMA
        self.dma_cnt_on_sem = [0] * self.NDMA
        self.all_dma = []
        self.seq = 0

    def add(self, eng, fn, r=(), w=(), dma=False, bg=False):
        op = _Op()
        op.eng, op.fn, op.sig, op.dma = eng, fn, False, dma
        op.seq = self.seq
        self.seq += 1
        deps = set()
        for k in r:
            lw = self.last_w.get(k)
            if lw is not None:
                deps.add(lw)
            lw = self.bg_w.get(k)
            if lw is not None:
                deps.add(lw)
        for k in w:
            lw = self.last_w.get(k)
            if lw is not None:
                deps.add(lw)
            for rd in self.readers.get(k, ()):
                deps.add(rd)
        if self.bar_dep[eng] is not None:
            deps.add(self.bar_dep[eng])
            self.bar_dep[eng] = None
        deps.discard(op)
        op.deps = [d for d in deps if not (d.eng == "pe" and eng == "pe" and not d.dma and not dma)]
        for k in r:
            lst = self.readers.setdefault(k, [])
            if not dma:
                lst[:] = [x for x in lst if x.dma or x.eng != eng]
            lst.append(op)
        if bg:
            for k in w:
                self.bg_w[k] = op
        else:
            for k in w:
                self.last_w[k] = op
                self.readers[k] = []
        if dma:
            if bg:
                s = self.NREG + self.bg_n % self.NBG
                self.bg_n += 1
            else:
                s = self.dma_n % self.NREG
                self.dma_n += 1
            prev = self.dma_last_on_sem[s]
            if prev is not None:
                op.deps.append(prev)
            self.dma_cnt_on_sem[s] += 16
            op.semi, op.cnt = s, self.dma_cnt_on_sem[s]
            self.dma_last_on_sem[s] = op
            if not bg:
                self.dma_since_barrier.append(op)
            self.all_dma.append(op)
        self.ops[eng].append(op)
        self.last_op[eng] = op
        return op

    def barrier(self):
        deps = [self.last_op[e] for e in ("pe", "act", "dve", "pool") if self.last_op[e] is not None]
        seen = {}
        for d in self.dma_since_barrier:
            seen[d.semi] = d
        deps += list(seen.values())
        m = _Op()
        m.eng, m.fn, m.sig, m.dma = "sp", (lambda e: e.nop()), False, False
        m.seq = self.seq
        self.seq += 1
        m.deps = deps
        self.ops["sp"].append(m)
        self.last_op["sp"] = m
        for e in ("pe", "act", "dve", "pool"):
            self.bar_dep[e] = m
        self.last_w = {}
        self.readers = {}
        self.dma_since_barrier = []

    def emit(self, nc, block, eng_sems, dma_sems):
        for e in self.ENGS:
            for op in self.ops[e]:
                for d in op.deps:
                    if not d.dma:
                        d.sig = True
        for e in self.ENGS:
            c = 0
            for op in self.ops[e]:
                if not op.dma and op.sig:
                    c += 1
                    op.cnt = c
        handles = {"pe": block.tensor, "act": block.scalar, "dve": block.vector, "pool": block.gpsimd,
                   "sp": block.sync}
        final_waits = [(s, self.dma_cnt_on_sem[s]) for s in range(self.NDMA) if self.dma_cnt_on_sem[s] > 0]

        def make(ename):
            ops = self.ops[ename]

            def body(e):
                waited = {}
                for op in ops:
                    need = {}
                    for d in op.deps:
                        if d.dma:
                            key, val, sem = ("d", d.semi), d.cnt, dma_sems[d.semi]
                        else:
                            key, val, sem = ("e", d.eng), d.cnt, eng_sems[d.eng]
                        if waited.get(key, 0) >= val:
                            continue
                        if key not in need or need[key][0] < val:
                            need[key] = (val, sem)
                    for key, (val, sem) in need.items():
                        e.wait_ge(sem, val)
                        waited[key] = val
                    ins = op.fn(e)
                    if op.dma:
                        ins.then_inc(dma_sems[op.semi], 16)
                    elif op.sig:
                        ins.then_inc(eng_sems[ename], 1)
                if ename == "sp":
                    for s, v in final_waits:
                        e.wait_ge(dma_sems[s], v)
            return body

        for ename in self.ENGS:
            handles[ename](make(ename))


class Cfg:
    def __init__(self, ncores=4, depth=2, nps=8, ls=2048, debug=False):
        self.ncores, self.depth, self.nps, self.ls, self.debug = ncores, depth, nps, ls, debug
        self.tp = nps * LP
        self.T = self.tp + ls


def _table_names(L):
    return [f"fn_cl{L}", f"fn_nsl{L}", f"hy_cf{L}", f"hy_sf{L}", f"hy_cft{L}", f"hy_sft{L}", f"hy_feat{L}",
            f"hy_tneg{L}"]


def make_tables(ls):
    t = {}
    c = np.arange(256, dtype=np.float64)
    ang = 2 * np.pi * np.outer(c, c) / 256.0
    t["fn_cw"] = np.cos(ang)
    t["fn_sw"] = np.sin(ang)
    for L in sorted({LP, ls}):
        p = np.arange(L, dtype=np.float64)
        a = 2 * np.pi * np.outer(p, p) / L
        sc = 1.0 / math.sqrt(L * 256.0)
        t[f"fn_cl{L}"] = np.cos(a) * sc
        t[f"fn_nsl{L}"] = -np.sin(a) * sc
        w = np.pi * np.outer(p, 2 * p + 1) / (2.0 * L)
        t[f"hy_cf{L}"] = np.cos(w)
        t[f"hy_sf{L}"] = np.sin(w)
        t[f"hy_cft{L}"] = np.cos(w).T / L
        t[f"hy_sft{L}"] = np.sin(w).T / L
        tt = p / L
        bands = np.arange(1, 17, dtype=np.float64)
        feats = np.concatenate([tt[:, None], np.cos(2 * np.pi * tt[:, None] * bands),
                                np.sin(2 * np.pi * tt[:, None] * bands)], axis=1)
        t[f"hy_feat{L}"] = feats.T
        t[f"hy_tneg{L}"] = (-tt).reshape(L // 128, 128).T
    deltas = np.abs(np.linspace(HY_MIN_DECAY, HY_MAX_DECAY, 1024))
    t["hy_delta"] = np.broadcast_to(deltas[None, :], (128, 1024))
    rows = ls // 64
    row = np.repeat(np.arange(rows, dtype=np.float64), 64)
    col = np.tile(np.arange(64, dtype=np.float64), rows)
    inv = 10000.0 ** (-np.arange(32, dtype=np.float64) / 32.0)
    angr = np.concatenate([row[:, None] * inv, col[:, None] * inv], axis=1)
    t["rope_cos"] = np.concatenate([np.cos(angr).T, np.cos(angr).T], axis=0)
    t["rope_sin"] = np.concatenate([np.sin(angr).T, np.sin(angr).T], axis=0)
    rm = np.zeros((128, 128))
    for d in range(64):
        rm[d, d + 64] = 1.0
        rm[d + 64, d] = -1.0
    t["rope_rm"] = rm
    t["ident"] = np.eye(128)
    sel = np.zeros((2, 2, 128))
    sel[0, 0, :] = 1.0
    sel[1, 1, :] = 1.0
    t["sel"] = sel.reshape(2, 256)
    return {k: np.ascontiguousarray(v, dtype=np.float32) for k, v in t.items()}


def build_program(cfg):
    nc = bass.Bass("TRN2", target_bir_lowering=False)
    S = Sched()
    DEPTH, NPS, LS, TP, T = cfg.depth, cfg.nps, cfg.ls, cfg.tp, cfg.T
    Ls = sorted({LP, LS})

    def din(name, shape):
        return nc.dram_tensor(name, list(shape), F32, kind="ExternalInput").ap()

    def dout(name, shape):
        return nc.dram_tensor(name, list(shape), F32, kind="ExternalOutput").ap()

    def dscr(name, shape, dt):
        return nc.dram_tensor(name, list(shape), dt, kind="Internal").ap()

    xp = din("xp", [TP, D])
    xs = din("xs", [LS, D])
    ck = din("ck", [DEPTH, PAST, NKV * HD])
    cv = din("cv", [DEPTH, PAST, NKV * HD])
    cond = din("cond", [2, D])
    w_mod = din("w_mod", [DEPTH, D, 3 * D])
    b_mod = din("b_mod", [DEPTH, 3 * D])
    g_pre = din("g_pre", [DEPTH, D])
    w_in = din("w_in", [DEPTH, D, IN_W])
    q_norm = din("q_norm", [DEPTH, HD])
    k_norm = din("k_norm", [DEPTH, HD])
    hy_sw = din("hy_short_w", [DEPTH, 3, 3072])
    hy_sb = din("hy_short_b", [DEPTH, 3072])
    hy_w1 = din("hy_ffn_w1", [DEPTH, 33, 64])
    hy_b1 = din("hy_ffn_b1", [DEPTH, 64])
    hy_w2 = din("hy_ffn_w2", [DEPTH, 64, 64])
    hy_b2 = din("hy_ffn_b2", [DEPTH, 64])
    hy_w3 = din("hy_ffn_w3", [DEPTH, 64, 4096])
    hy_b3 = din("hy_ffn_b3", [DEPTH, 4096])
    hy_fr = din("hy_sin_freq", [DEPTH, 64])
    hy_bias = din("hy_bias", [DEPTH, 2, 1024])
    w_ao = din("w_attn_o", [DEPTH, 2048, D])
    w_fo = din("w_fnet_o", [DEPTH, 1024, D])
    w_ho = din("w_hy_o", [DEPTH, 1024, D])
    w_out = din("w_out", [DEPTH, D, D])
    g_post = din("g_post", [DEPTH, D])
    tabs = {}
    tshapes = {"fn_cw": [256, 256], "fn_sw": [256, 256], "hy_delta": [128, 1024], "rope_cos": [128, LS],
               "rope_sin": [128, LS], "rope_rm": [128, 128], "ident": [128, 128], "sel": [2, 256]}
    for L in Ls:
        for n in _table_names(L)[:6]:
            tshapes[n] = [L, L]
        tshapes[f"hy_feat{L}"] = [33, L]
        tshapes[f"hy_tneg{L}"] = [128, L // 128]
    for n, sh in tshapes.items():
        tabs[n] = din(n, sh)

    y_p = dout("y_p", [TP, D])
    y_s = dout("y_s", [LS, D])
    nck = dout("nck", [NPS, DEPTH, LP, NKV * HD])
    ncv = dout("ncv", [NPS, DEPTH, LP, NKV * HD])

    wb_mod = [dscr(f"wb_mod{l}", [24, 128, KC, 512], BF16) for l in range(DEPTH)]
    wb_in = [dscr(f"wb_in{l}", [46, 128, KC, 512], BF16) for l in range(DEPTH)]
    wb_o = [dscr(f"wb_o{l}", [8, 128, KC, 512], BF16) for l in range(DEPTH)]
    wb_out = [dscr(f"wb_out{l}", [8, 128, KC, 512], BF16) for l in range(DEPTH)]
    projT = dscr("projT", [IN_W, T], BF16)
    bT = dscr("bT", [D, T], BF16)
    xres = dscr("xres", [T, D], F32)
    ggb = dscr("ggb", [2, 128, D], F32)
    gspec = {L: dscr(f"gspec{L}", [2, 2, L, 1024], BF16) for L in Ls}
    if cfg.debug:
        dbg_proj = nc.dram_tensor("dbg_proj", [IN_W, T], BF16, kind="ExternalOutput").ap()
        dbg_bT = nc.dram_tensor("dbg_bT", [D, T], BF16, kind="ExternalOutput").ap()

    ARENA_W = 45056
    ctx = []
    arena_t = nc.sbuf_tensor("arena", [128, ARENA_W], F32)
    psum_t = nc.psum_tensor("psum", [128, 8, 512], F32)
    arena = arena_t.__enter__()
    ps = psum_t.__enter__()

    class Arena:
        def __init__(self, base=0):
            self.off = base

        def f32(self, n):
            a = arena[:, self.off:self.off + n]
            self.off += n
            assert self.off <= ARENA_W, f"arena overflow {self.off}"
            return a

        def bf16(self, n):
            assert n % 2 == 0
            a = arena[:, self.off:self.off + n // 2].bitcast(BF16)
            self.off += n // 2
            assert self.off <= ARENA_W, f"arena overflow {self.off}"
            return a

    def psb(b):
        return ps[:, b, :]

    def psb16(b):
        return ps[:, b, :].bitcast(BF16)

    def MM(out, lhsT, rhs, start, stop, r, w):
        S.add("pe", lambda e: e.matmul(out, lhsT, rhs, start=start, stop=stop), r, w)

    def TR(out, in_, ident, r, w):
        S.add("pe", lambda e: e.transpose(out, in_, ident), r, w)

    def ACT(out, in_, func, r, w, bias=None, scale=None, accum=None):
        kw = {}
        if bias is not None:
            kw["bias"] = bias
        if scale is not None:
            kw["scale"] = scale
        if accum is not None:
            kw["accum_out"] = accum
        S.add("act", lambda e: e.activation(out=out, in_=in_, func=func, **kw), r, w)

    def DMA(q, out, in_, r, w, slow=False, bg=False):
        if bg:
            S.add(q, lambda e: e.dma_start(out=out, in_=in_), r, w, dma=True, bg=True)
        elif slow:
            S.add(q, lambda e: e.dma_start(out=out, in_=in_, allow_slow_non_contiguous=True), r, w, dma=True)
        else:
            S.add(q, lambda e: e.dma_start(out=out, in_=in_), r, w, dma=True)

    def TTo(eng, out, a, b, op, r, w):
        S.add(eng, lambda e: e.tensor_tensor(out=out, in0=a, in1=b, op=op), r, w)

    def TS(eng, out, a, s1, s2, op0, op1, r, w):
        if op1 is None:
            S.add(eng, lambda e: e.tensor_scalar(out=out, in0=a, scalar1=s1, scalar2=None, op0=op0), r, w)
        else:
            S.add(eng, lambda e: e.tensor_scalar(out=out, in0=a, scalar1=s1, scalar2=s2, op0=op0, op1=op1), r, w)

    def STT(out, in0, scalar, in1, op0, op1, r, w):
        S.add("dve", lambda e: e.scalar_tensor_tensor(out=out, in0=in0, scalar=scalar, in1=in1, op0=op0, op1=op1),
              r, w)

    def CP(eng, out, in_, r, w):
        if eng == "act":
            S.add("act", lambda e: e.copy(out=out, in_=in_), r, w)
        else:
            S.add(eng, lambda e: e.tensor_copy(out=out, in_=in_), r, w)

    def RECIP(out, in_, r, w):
        S.add("dve", lambda e: e.reciprocal(out=out, in_=in_), r, w)

    def RSQRT_ACT(out, in_, mul, r, w):
        ACT(out, in_, AF.Ln, r, w, bias=EPS_AP(out), scale=mul)
        ACT(out, out, AF.Exp, w, w, scale=-0.5)

    def MSET(eng, ap, val, r, w):
        S.add(eng, lambda e: e.memset(ap, val), r, w)

    PA = Arena(0)
    ident_bf = PA.bf16(128)
    ident_f = PA.f32(128)
    ones_bf = PA.bf16(128)
    ones_f = PA.f32(128)
    rm_bf = PA.bf16(128)
    sel_f = PA.f32(256)
    condT = PA.bf16(64).rearrange("p (k j) -> p k j", j=2)
    modT = PA.f32(128).rearrange("p (k j) -> p k j", j=2)
    qn_col = PA.f32(2 * DEPTH)
    eps_col = PA.f32(1)

    def EPS_AP(out):
        return eps_col[0:out.shape[0], 0:1]
    PERSIST = PA.off
    DMA("pool", ident_bf, tabs["ident"], [], ["ident_bf"])
    DMA("sp", ident_f, tabs["ident"], [], ["ident_f"])
    DMA("pool", rm_bf, tabs["rope_rm"], [], ["rm_bf"])
    DMA("sp", sel_f[0:2, :], tabs["sel"], [], ["sel_f"])
    MSET("dve", ones_bf, 1.0, [], ["ones_bf"])
    MSET("dve", ones_f, 1.0, [], ["ones_f"])
    MSET("dve", eps_col, EPS, [], ["eps_col"])
    for l in range(DEPTH):
        DMA("sp", qn_col[:, 2 * l:2 * l + 1], q_norm[l].rearrange("(d o) -> d o", o=1), [], [f"qnp{l}"], slow=True)
        DMA("sp", qn_col[:, 2 * l + 1:2 * l + 2], k_norm[l].rearrange("(d o) -> d o", o=1), [], [f"knp{l}"], slow=True)
    A0 = Arena(PERSIST)
    cond_f = A0.f32(64).rearrange("p (k j) -> p k j", j=2)
    for j in range(2):
        DMA("sp", cond_f[:, :, j], cond[j].rearrange("(k p) -> p k", p=128), [], [f"cond_f{j}"], slow=True)
    ACT(condT, cond_f, AF.Silu, ["cond_f0", "cond_f1"], ["condT"])

    def cast_weights(l):
        for cb in range(24):
            DMA("pool", wb_mod[l][cb], w_mod[l][:, cb * 512:(cb + 1) * 512].rearrange("(k p) n -> p k n", p=128),
                [], [f"wbmod{l}_{cb}"], bg=True)
        for cb in range(46):
            DMA("pool", wb_in[l][cb], w_in[l][:, cb * 512:(cb + 1) * 512].rearrange("(k p) n -> p k n", p=128),
                [], [f"wbin{l}_{cb}"], bg=True)
        for cb in range(8):
            cs = slice(cb * 512, (cb + 1) * 512)
            DMA("pool", wb_o[l][cb][:, 0:16, :], w_ao[l][:, cs].rearrange("(k p) n -> p k n", p=128), [], [f"wbo{l}_{cb}a"], bg=True)
            DMA("pool", wb_o[l][cb][:, 16:24, :], w_fo[l][:, cs].rearrange("(k p) n -> p k n", p=128), [], [f"wbo{l}_{cb}f"], bg=True)
            DMA("pool", wb_o[l][cb][:, 24:32, :], w_ho[l][:, cs].rearrange("(k p) n -> p k n", p=128), [], [f"wbo{l}_{cb}h"], bg=True)
        for cb in range(8):
            DMA("pool", wb_out[l][cb], w_out[l][:, cb * 512:(cb + 1) * 512].rearrange("(k p) n -> p k n", p=128),
                [], [f"wbout{l}_{cb}"], bg=True)

    for l in range(DEPTH):
        cast_weights(l)
    S.barrier()

    tiles = []
    for i in range(TP // TT_):
        tiles.append(("p", i * TT_, 0))
    for i in range(LS // TT_):
        tiles.append(("s", i * TT_, 1))

    def x_rows(l, kind, off, n):
        if l == 0:
            return (xp if kind == "p" else xs)[off:off + n, :]
        g = off if kind == "p" else TP + off
        return xres[g:g + n, :]

    def y_rows(l, kind, off, n):
        if l == DEPTH - 1:
            return (y_p if kind == "p" else y_s)[off:off + n, :]
        g = off if kind == "p" else TP + off
        return xres[g:g + n, :]

    def tok0(kind, off):
        return off if kind == "p" else TP + off

    def stage_mod(l):
        A = Arena(PERSIST)
        modrow = A.f32(3 * D)
        bm2 = A.f32(3 * D)
        base2 = A.off
        wblk = [A.bf16(KC * 512).rearrange("p (k n) -> p k n", n=512) for _ in range(2)]
        for j in range(2):
            DMA("sp", bm2[j:j + 1, :], b_mod[l].rearrange("(o n) -> o n", o=1), [], [f"bm2_{j}"])
        DMA("sp", wblk[0], wb_mod[l][0], [f"wbmod{l}_0"], ["wblk0"])
        for cb in range(24):
            wb = wblk[cb % 2]
            if cb + 1 < 24:
                DMA("sp", wblk[(cb + 1) % 2], wb_mod[l][cb + 1], [f"wbmod{l}_{cb + 1}"], [f"wblk{(cb + 1) % 2}"])
            pb = cb % 2
            for k in range(KC):
                MM(psb(pb)[0:2, :], condT[:, k, :], wb[:, k, :], k == 0, k == KC - 1,
                   ["condT", f"wblk{cb % 2}"], [f"ps{pb}"])
            TTo("dve", modrow[0:2, cb * 512:(cb + 1) * 512], psb(pb)[0:2, :], bm2[0:2, cb * 512:(cb + 1) * 512],
                ALU.add, [f"ps{pb}", "bm2_0", "bm2_1"], ["modrow"])
        S.barrier()
        B = Arena(base2)
        gp2 = B.f32(D)
        gq2 = B.f32(D)
        s1row = B.f32(D)
        ggrow = B.f32(D)
        ggs = [B.f32(512) for _ in range(2)]
        for j in range(2):
            DMA("sp", gp2[j:j + 1, :], g_pre[l].rearrange("(o n) -> o n", o=1), [], [f"gp2_{j}"])
            DMA("sp", gq2[j:j + 1, :], g_post[l].rearrange("(o n) -> o n", o=1), [], [f"gq2_{j}"])
        STT(s1row[0:2, :], modrow[0:2, D:2 * D], 1.0, gp2[0:2, :], ALU.add, ALU.mult,
            ["modrow", "gp2_0", "gp2_1"], ["s1row"])
        TTo("dve", ggrow[0:2, :], modrow[0:2, 2 * D:3 * D], gq2[0:2, :], ALU.mult, ["modrow", "gq2_0", "gq2_1"],
            ["ggrow"])
        pmt = psb(2)[:, 0:128].rearrange("p (k j) -> p k j", j=2)
        for k in range(KC):
            TR(pmt[:, k, :], s1row[0:2, k * 128:(k + 1) * 128], ident_f[0:2, 0:2], ["s1row", "ident_f"], ["ps2"])
        for k in range(KC):
            TR(pmt[:, KC + k, :], modrow[0:2, k * 128:(k + 1) * 128], ident_f[0:2, 0:2], ["modrow", "ident_f"],
               ["ps2"])
        CP("dve", modT, pmt, ["ps2"], ["modT"])
        n = 0
        for j in range(2):
            for cb in range(8):
                pb = 4 + n % 2
                MM(psb(pb), sel_f[0:2, j * 128:(j + 1) * 128], ggrow[0:2, cb * 512:(cb + 1) * 512], True, True,
                   ["sel_f", "ggrow"], [f"ps{pb}"])
                CP("dve", ggs[n % 2], psb(pb), [f"ps{pb}"], [f"ggs{n % 2}"])
                DMA("sp", ggb[j][:, cb * 512:(cb + 1) * 512], ggs[n % 2], [f"ggs{n % 2}"], [f"ggb{j}"])
                n += 1
        S.barrier()

    def col_func(c0):
        if C_AG <= c0 < C_FI or C_FG <= c0 < C_HV or C_HG <= c0 < C_GA:
            return AF.Silu
        if c0 >= C_GA:
            return AF.Sigmoid
        return None

    def stage_proj(l, kind, off, j):
        A = Arena(PERSIST)
        hT = A.bf16(KC * 512).rearrange("p (k n) -> p k n", n=512)
        xn = A.bf16(4 * D).rearrange("p (s n) -> p s n", n=D)
        xf = [A.f32(D) for _ in range(2)]
        junk = A.bf16(D)
        ssq = A.f32(4)
        rstd = A.f32(4)
        wblk = [A.bf16(KC * 512).rearrange("p (k n) -> p k n", n=512) for _ in range(2)]
        ost = [A.bf16(512) for _ in range(4)]
        t0 = tok0(kind, off)
        DMA("sp", wblk[0], wb_in[l][0], [f"wbin{l}_0"], ["wblk0"])
        for s in range(4):
            xb = xf[s % 2]
            DMA("sp", xb, x_rows(l, kind, off + s * 128, 128), [], [f"xf{s % 2}"])
            ACT(junk, xb, AF.Square, [f"xf{s % 2}"], ["junk", f"ssq{s}"], accum=ssq[:, s:s + 1])
            RSQRT_ACT(rstd[:, s:s + 1], ssq[:, s:s + 1], 1.0 / D, [f"ssq{s}"], [f"rstd{s}"])
            TS("dve", xn[:, s, :], xb, rstd[:, s:s + 1], None, ALU.mult, None, [f"xf{s % 2}", f"rstd{s}"],
               [f"xn{s}"])
        for k in range(KC):
            pb = k % 2
            pv = psb16(pb)[:, 0:512]
            for s in range(4):
                TR(pv[:, s * 128:(s + 1) * 128], xn[:, s, k * 128:(k + 1) * 128], ident_bf, [f"xn{s}", "ident_bf"],
                   [f"ps{pb}"])
            ACT(hT[:, k, :], pv, AF.Identity, [f"ps{pb}", "modT"], [f"hT{k}"], bias=modT[:, KC + k, j:j + 1],
                scale=modT[:, k, j:j + 1])
        hkeys = [f"hT{k}" for k in range(KC)]
        n = 0
        for cb in range(46):
            wb = wblk[cb % 2]
            if cb + 1 < 46:
                DMA("sp", wblk[(cb + 1) % 2], wb_in[l][cb + 1], [f"wbin{l}_{cb + 1}"], [f"wblk{(cb + 1) % 2}"])
            for q in range(4):
                c0 = cb * 512 + q * 128
                pb = 2 + n % 6
                for k in range(KC):
                    MM(psb(pb), wb[:, k, q * 128:(q + 1) * 128], hT[:, k, :], k == 0, k == KC - 1,
                       [f"wblk{cb % 2}", f"hT{k}"], [f"ps{pb}"])
                o = ost[n % 4]
                fn = col_func(c0)
                if fn is None:
                    CP("dve", o, psb(pb), [f"ps{pb}"], [f"ost{n % 4}"])
                else:
                    ACT(o, psb(pb), fn, [f"ps{pb}"], [f"ost{n % 4}"])
                DMA("sp", projT[c0:c0 + 128, t0:t0 + 512], o, [f"ost{n % 4}"], [f"projT_{c0}_{t0}"])
                n += 1
        S.barrier()

    def stage_attn(l, t0, L, rope, seq_out):
        A = Arena(PERSIST)
        Lk = L + (PAST if rope else 0)
        nkc = Lk // 128
        QT = min(512, L)
        kT = A.bf16(NKV * Lk).rearrange("p (h n) -> p h n", n=Lk)
        V = A.bf16(nkc * NKV * HD).rearrange("p (c h d) -> p c h d", h=NKV, d=HD)
        if rope:
            cosT = A.bf16(L)
            sinT = A.bf16(L)
            DMA("pool", cosT, tabs["rope_cos"], [], ["cosT"])
            DMA("pool", sinT, tabs["rope_sin"], [], ["sinT"])
            ckb = A.bf16(4 * 512).rearrange("p (c n) -> p c n", n=512)
            DMA("pool", ckb, ck[l].rearrange("(c p) n -> p c n", p=128), [], ["ckb"])
            DMA("pool", V[:, L // 128:nkc, :, :].rearrange("p c h d -> p c (h d)"),
                cv[l].rearrange("(c p) n -> p c n", p=128), [], ["Vctx"])
        raw = [A.bf16(QT) for _ in range(2)]
        sq = [A.bf16(QT) for _ in range(2)]
        rs = [A.f32(QT) for _ in range(2)]
        qn = [A.bf16(QT) for _ in range(2)]
        t1 = [A.f32(QT) for _ in range(2)]
        t2 = [A.f32(QT) for _ in range(2)]
        qT = [A.bf16(QT) for _ in range(2)]
        ag = [A.bf16(QT) for _ in range(2)]
        pT = [A.bf16(QT) for _ in range(3)]
        rd = [A.f32(QT) for _ in range(2)]
        ot = [A.f32(QT) for _ in range(2)]
        ob = [A.bf16(QT) for _ in range(2)]
        vraw = [A.bf16(L) for _ in range(2)]
        cst = [A.f32(512) for _ in range(2)]
        cnt = {"n": 0}

        def nr_load(src_rows, tq0, n):
            i = cnt["n"] % 2
            cnt["n"] += 1
            DMA("sp", raw[i][:, 0:n], projT[src_rows:src_rows + 128, t0 + tq0:t0 + tq0 + n], [], [f"raw{i}"])
            return i

        def norm_rope(src_rows, tq0, n, gcol, dst, dst_key, pre=None):
            i = nr_load(src_rows, tq0, n) if pre is None else pre
            R, SQ, RS, QN, T1, T2 = raw[i], sq[i], rs[i], qn[i], t1[i], t2[i]
            ACT(SQ[:, 0:n], R[:, 0:n], AF.Square, [f"raw{i}"], [f"sq{i}"])
            MM(psb(0)[:, 0:n], ones_bf, SQ[:, 0:n], True, True, ["ones_bf", f"sq{i}"], ["ps0"])
            RSQRT_ACT(RS[:, 0:n], psb(0)[:, 0:n], 1.0 / HD, ["ps0"], [f"rs{i}"])
            tgt = QN if rope else dst
            tkey = f"qn{i}" if rope else dst_key
            STT(tgt[:, 0:n] if rope else tgt, R[:, 0:n], gcol, RS[:, 0:n], ALU.mult, ALU.mult,
                [f"raw{i}", f"rs{i}"], [tkey])
            if rope:
                MM(psb(1)[:, 0:n], rm_bf, QN[:, 0:n], True, True, ["rm_bf", f"qn{i}"], ["ps1"])
                TTo("pool", T1[:, 0:n], QN[:, 0:n], cosT[:, tq0:tq0 + n], ALU.mult, [f"qn{i}", "cosT"], [f"t1{i}"])
                TTo("dve", T2[:, 0:n], psb(1)[:, 0:n], sinT[:, tq0:tq0 + n], ALU.mult, ["ps1", "sinT"], [f"t2{i}"])
                TTo("dve", dst, T1[:, 0:n], T2[:, 0:n], ALU.add, [f"t1{i}", f"t2{i}"], [dst_key])
            return QN if rope else dst

        ncq = 0
        import os as _os
        _cut = _os.environ.get("DBG_ATTN_CUT", "")
        for h in range(NKV):
            for tq in range(L // QT):
                norm_rope(C_K + h * 128, tq * QT, QT, qn_col[:, 2 * l + 1:2 * l + 2],
                          kT[:, h, tq * QT:(tq + 1) * QT], f"kT{h}")
            if _cut == "prepA":
                continue
            if seq_out is not None:
                for c in range(L // 128):
                    pv = psb16(2)[:, 0:128]
                    TR(pv, kT[:, h, c * 128:(c + 1) * 128], ident_bf, [f"kT{h}", "ident_bf"], ["ps2"])
                    cs = cst[ncq % 2]
                    CP("act", cs[:, 0:128], pv, ["ps2"], [f"cst{ncq % 2}"])
                    DMA("sp", nck[seq_out, l, c * 128:(c + 1) * 128, h * 128:(h + 1) * 128], cs[:, 0:128],
                        [f"cst{ncq % 2}"], [f"nck{h}_{c}"])
                    ncq += 1
            if rope:
                for c in range(4):
                    pv = psb16(2)[:, 0:128]
                    TR(pv, ckb[:, c, h * 128:(h + 1) * 128], ident_bf, ["ckb", "ident_bf"], ["ps2"])
                    CP("act", kT[:, h, L + c * 128:L + (c + 1) * 128], pv, ["ps2"], [f"kT{h}"])
            if _cut == "prepB":
                continue
            vr = vraw[h % 2]
            DMA("sp", vr, projT[C_V + h * 128:C_V + (h + 1) * 128, t0:t0 + L], [], [f"vraw{h % 2}"])
            for c in range(L // 128):
                pv = psb16(3)[:, 0:128]
                TR(pv, vr[:, c * 128:(c + 1) * 128], ident_bf, [f"vraw{h % 2}", "ident_bf"], ["ps3"])
                CP("act", V[:, c, h, :], pv, ["ps3"], [f"V{h}"])
                if seq_out is not None:
                    cs = cst[ncq % 2]
                    CP("dve", cs[:, 0:128], V[:, c, h, :], [f"V{h}"], [f"cst{ncq % 2}"])
                    DMA("sp", ncv[seq_out, l, c * 128:(c + 1) * 128, h * 128:(h + 1) * 128], cs[:, 0:128],
                        [f"cst{ncq % 2}"], [f"ncv{h}_{c}"])
                    ncq += 1
        it = 0
        npt = 0
        iters = [(h, g, tq) for h in range(NKV) for g in range(4) for tq in range(L // QT)]

        def pre_load(idx):
            h_, g_, tq_ = iters[idx]
            hd_ = h_ * 4 + g_
            ri = nr_load(C_Q + hd_ * 128, tq_ * QT, QT)
            DMA("sp", ag[idx % 2], projT[C_AG + hd_ * 128:C_AG + (hd_ + 1) * 128, t0 + tq_ * QT:t0 + (tq_ + 1) * QT],
                [], [f"ag{idx % 2}"])
            return ri

        pend = pre_load(0)
        for idx, (h, g, tq) in enumerate(iters):
            if True:
                hd = h * 4 + g
                if True:
                    i = it % 2
                    it += 1
                    cur = pend
                    if idx + 1 < len(iters):
                        pend = pre_load(idx + 1)
                    norm_rope(C_Q + hd * 128, tq * QT, QT, qn_col[:, 2 * l:2 * l + 1], qT[i], f"qT{i}", pre=cur)
                    po, pd = 4 + i, 6 + i
                    vkeys = [f"V{h}"] + (["Vctx"] if rope else [])
                    for c in range(nkc):
                        pss = 2 + c % 2
                        MM(psb(pss)[:, 0:QT], kT[:, h, c * 128:(c + 1) * 128], qT[i], True, True,
                           [f"kT{h}", f"qT{i}"], [f"ps{pss}"])
                        P = pT[npt % 3]
                        pk = f"pT{npt % 3}"
                        npt += 1
                        ACT(P, psb(pss)[:, 0:QT], AF.Exp, [f"ps{pss}"], [pk], scale=HD ** -0.5)
                        MM(psb(po)[:, 0:QT], V[:, c, h, :], P, c == 0, c == nkc - 1, vkeys + [pk], [f"ps{po}"])
                        MM(psb(pd)[:, 0:QT], ones_bf, P, c == 0, c == nkc - 1, ["ones_bf", pk], [f"ps{pd}"])
                    RECIP(rd[i], psb(pd)[:, 0:QT], [f"ps{pd}"], [f"rd{i}"])
                    TTo("dve", ot[i], psb(po)[:, 0:QT], rd[i], ALU.mult, [f"ps{po}", f"rd{i}"], [f"ot{i}"])
                    TTo("dve" if _cut == "nopool" else "pool", ob[i], ot[i], ag[i], ALU.mult, [f"ot{i}", f"ag{i}"], [f"ob{i}"])
                    DMA("sp", bT[hd * 128:(hd + 1) * 128, t0 + tq * QT:t0 + (tq + 1) * QT], ob[i], [f"ob{i}"],
                        [f"bT_a{hd}_{tq}"])
        S.barrier()

    def stage_fnet(l, t0, L):
        A = Arena(PERSIST)
        QT = min(512, L)
        nt = L // 128
        UT = A.bf16(8 * L).rearrange("p (c n) -> p c n", n=L)
        cw = A.bf16(512).rearrange("p (c n) -> p c n", n=256)
        sw = A.bf16(512).rearrange("p (c n) -> p c n", n=256)
        Asb = A.bf16(nt * 1024).rearrange("p (t n) -> p t n", n=1024)
        Bsb = A.bf16(nt * 1024).rearrange("p (t n) -> p t n", n=1024)
        clt = [A.bf16(nt * QT).rearrange("p (t n) -> p t n", n=QT) for _ in range(2)]
        slt = [A.bf16(nt * QT).rearrange("p (t n) -> p t n", n=QT) for _ in range(2)]
        fg = [A.bf16(QT) for _ in range(2)]
        ost = [A.bf16(QT) for _ in range(2)]
        DMA("sp", UT, projT[C_FI:C_FI + 1024, t0:t0 + L].rearrange("(c p) t -> p c t", p=128), [], ["UT"])
        DMA("pool", cw, tabs["fn_cw"].rearrange("(c p) n -> p c n", p=128), [], ["cw"])
        DMA("pool", sw, tabs["fn_sw"].rearrange("(c p) n -> p c n", p=128), [], ["sw"])
        for tt in range(nt):
            pa, pb_ = (0, 2) if tt % 2 == 0 else (4, 6)
            for g in range(4):
                bank, o = divmod(g * 256, 512)
                for hf in range(2):
                    MM(psb(pa + bank)[:, o:o + 256], UT[:, g * 2 + hf, tt * 128:(tt + 1) * 128], cw[:, hf, :],
                       hf == 0, hf == 1, ["UT", "cw"], [f"ps{pa + bank}"])
                for hf in range(2):
                    MM(psb(pb_ + bank)[:, o:o + 256], UT[:, g * 2 + hf, tt * 128:(tt + 1) * 128], sw[:, hf, :],
                       hf == 0, hf == 1, ["UT", "sw"], [f"ps{pb_ + bank}"])
            for bank in range(2):
                CP("act", Asb[:, tt, bank * 512:(bank + 1) * 512], psb(pa + bank), [f"ps{pa + bank}"], [f"Asb{tt}"])
                CP("dve", Bsb[:, tt, bank * 512:(bank + 1) * 512], psb(pb_ + bank), [f"ps{pb_ + bank}"],
                   [f"Bsb{tt}"])
        akeys = [f"Asb{tt}" for tt in range(nt)]
        bkeys = [f"Bsb{tt}" for tt in range(nt)]
        n = 0
        for tq in range(L // QT):
            i = tq % 2
            DMA("pool", clt[i], tabs[f"fn_cl{L}"][:, tq * QT:(tq + 1) * QT].rearrange("(t p) n -> p t n", p=128), [],
                [f"clt{i}"])
            DMA("pool", slt[i], tabs[f"fn_nsl{L}"][:, tq * QT:(tq + 1) * QT].rearrange("(t p) n -> p t n", p=128),
                [], [f"slt{i}"])
            for c in range(8):
                pb = n % 4
                for tch in range(nt):
                    MM(psb(pb)[:, 0:QT], Asb[:, tch, c * 128:(c + 1) * 128], clt[i][:, tch, :], tch == 0, False,
                       [f"Asb{tch}", f"clt{i}"], [f"ps{pb}"])
                for tch in range(nt):
                    MM(psb(pb)[:, 0:QT], Bsb[:, tch, c * 128:(c + 1) * 128], slt[i][:, tch, :], False,
                       tch == nt - 1, [f"Bsb{tch}", f"slt{i}"], [f"ps{pb}"])
                o = ost[n % 2]
                DMA("sp", fg[n % 2], projT[C_FG + c * 128:C_FG + (c + 1) * 128, t0 + tq * QT:t0 + (tq + 1) * QT], [],
                    [f"fg{n % 2}"])
                TTo("dve", o, psb(pb)[:, 0:QT], fg[n % 2], ALU.mult, [f"ps{pb}", f"fg{n % 2}"], [f"ost{n % 2}"])
                DMA("sp", bT[2048 + c * 128:2048 + (c + 1) * 128, t0 + tq * QT:t0 + (tq + 1) * QT], o,
                    [f"ost{n % 2}"], [f"bT_f{c}_{tq}"])
                n += 1
        S.barrier()

    def stage_filters(l, L):
        A = Arena(PERSIST)
        nt = L // 128
        CH = min(512, L)
        featT = A.f32(L)
        w1 = A.f32(64)
        w2 = A.f32(64)
        w3a = A.f32(4096)
        colp = A.f32(8)
        h1T = A.f32(L)
        h2a = A.f32(L)
        tmp = [A.f32(CH) for _ in range(2)]
        tmq = [A.f32(CH) for _ in range(2)]
        MAGIC = 12582912.0
        delta = A.f32(1024)
        tneg = A.f32(nt)
        dec = [A.f32(512) for _ in range(2)]
        hf = [A.f32(512) for _ in range(2)]
        hb = [A.f32(512) for _ in range(2)]
        ab = [A.f32(512) for _ in range(2)]
        ab2 = [A.f32(512) for _ in range(2)]
        hs = A.bf16(nt * 512).rearrange("p (t n) -> p t n", n=512)
        hdd = A.bf16(nt * 512).rearrange("p (t n) -> p t n", n=512)
        rn = A.f32(512)
        cft = [A.bf16(nt * 128).rearrange("p (t n) -> p t n", n=128) for _ in range(2)]
        sft = [A.bf16(nt * 128).rearrange("p (t n) -> p t n", n=128) for _ in range(2)]
        gst = [A.bf16(512) for _ in range(4)]
        DMA("sp", featT[0:33, :], tabs[f"hy_feat{L}"], [], ["featT"])
        DMA("sp", w1[0:33, :], hy_w1[l], [], ["w1"])
        DMA("sp", w2[0:64, :], hy_w2[l], [], ["w2"])
        DMA("sp", w3a[0:64, :], hy_w3[l], [], ["w3a"])
        DMA("sp", w3a[64:65, :], hy_b3[l].rearrange("(o n) -> o n", o=1), [], ["w3b"])
        DMA("sp", colp[0:64, 0:1], hy_fr[l].rearrange("(d o) -> d o", o=1), [], ["colp0"], slow=True)
        DMA("sp", colp[0:64, 1:2], hy_b1[l].rearrange("(d o) -> d o", o=1), [], ["colp1"], slow=True)
        DMA("sp", colp[0:64, 2:3], hy_b2[l].rearrange("(d o) -> d o", o=1), [], ["colp2"], slow=True)
        DMA("sp", delta, tabs["hy_delta"], [], ["delta"])
        DMA("sp", tneg, tabs[f"hy_tneg{L}"], [], ["tneg"])
        OFF = 17.0 * math.pi
        for q in (1, 2):
            TTo("dve", colp[0:64, 2 + q:3 + q], colp[0:64, q:q + 1], colp[0:64, 0:1], ALU.mult,
                ["colp0", f"colp{q}"], [f"colp{2 + q}"])
        MSET("dve", h2a[64:65, :], 1.0, [], ["h2ones"])
        for layer, (wt, kdim, src, dst, bcol, wk, sk, dk) in enumerate((
                (w1, 33, featT, h1T, 3, "w1", "featT", "h1T"), (w2, 64, h1T, h2a, 4, "w2", "h1T", "h2T"))):
            for c in range(L // CH):
                i = c % 2
                MM(psb(i)[0:64, 0:CH], wt[0:kdim, :], src[0:kdim, c * CH:(c + 1) * CH], True, True, [wk, sk],
                   [f"ps{i}"])
                TS("dve", tmp[i][0:64, :], psb(i)[0:64, 0:CH], colp[0:64, 0:1], colp[0:64, bcol:bcol + 1], ALU.mult,
                   ALU.add, [f"ps{i}", "colp0", f"colp{bcol}"], [f"tmp{i}"])
                TS("dve", tmq[i][0:64, :], tmp[i][0:64, :], 1.0 / TWO_PI, MAGIC, ALU.mult, ALU.add, [f"tmp{i}"],
                   [f"tmq{i}"])
                TS("dve", tmq[i][0:64, :], tmq[i][0:64, :], -MAGIC, None, ALU.add, None, [f"tmq{i}"], [f"tmq{i}"])
                STT(tmp[i][0:64, :], tmq[i][0:64, :], -TWO_PI, tmp[i][0:64, :], ALU.mult, ALU.add,
                    [f"tmq{i}", f"tmp{i}"], [f"tmp{i}"])
                ACT(dst[0:64, c * CH:(c + 1) * CH], tmp[i][0:64, :], AF.Sin, [f"tmp{i}"], [dk])
        n = 0
        ng = 0
        for o in range(2):
            for chf in range(2):
                cF = o * 2048 + chf * 512
                cB = o * 2048 + 1024 + chf * 512
                for pt in range(nt):
                    i = n % 2
                    n += 1
                    MM(psb(i), h2a[0:65, pt * 128:(pt + 1) * 128], w3a[0:65, cF:cF + 512], True, True,
                       ["h2T", "h2ones", "w3a", "w3b"], [f"ps{i}"])
                    MM(psb(2 + i), h2a[0:65, pt * 128:(pt + 1) * 128], w3a[0:65, cB:cB + 512], True, True,
                       ["h2T", "h2ones", "w3a", "w3b"], [f"ps{2 + i}"])
                    ACT(dec[i], delta[:, chf * 512:(chf + 1) * 512], AF.Exp, ["delta", "tneg"], [f"dec{i}"],
                        scale=tneg[:, pt:pt + 1])
                    TTo("dve", hf[i], psb(i), dec[i], ALU.mult, [f"ps{i}", f"dec{i}"], [f"hf{i}"])
                    TTo("dve", hb[i], psb(2 + i), dec[i], ALU.mult, [f"ps{2 + i}", f"dec{i}"], [f"hb{i}"])
                    STT(ab[i], hf[i], -1.0, hf[i], ALU.mult, ALU.max, [f"hf{i}"], [f"ab{i}"])
                    STT(ab2[i], hb[i], -1.0, hb[i], ALU.mult, ALU.max, [f"hb{i}"], [f"ab2{i}"])
                    TTo("pool", ab[i], ab[i], ab2[i], ALU.add, [f"ab{i}", f"ab2{i}"], [f"ab{i}"])
                    MM(psb(4), ones_f, ab[i], pt == 0, pt == nt - 1, ["ones_f", f"ab{i}"], ["ps4"])
                    TTo("pool", hs[:, pt, :], hf[i], hb[i], ALU.add, [f"hf{i}", f"hb{i}"], [f"hs{pt}"])
                    TTo("pool", hdd[:, pt, :], hf[i], hb[i], ALU.subtract, [f"hf{i}", f"hb{i}"], [f"hd{pt}"])
                TS("dve", rn, psb(4), EPS, None, ALU.add, None, ["ps4"], ["rn"])
                RECIP(rn, rn, ["rn"], ["rn"])
                for kt in range(nt):
                    i = kt % 2
                    DMA("pool", cft[i], tabs[f"hy_cf{L}"][:, kt * 128:(kt + 1) * 128].rearrange(
                        "(t p) n -> p t n", p=128), [], [f"cft{i}"])
                    DMA("pool", sft[i], tabs[f"hy_sf{L}"][:, kt * 128:(kt + 1) * 128].rearrange(
                        "(t p) n -> p t n", p=128), [], [f"sft{i}"])
                    pr, pi_ = 5 + 0, 6 + i % 2
                    pr = 5 if i == 0 else 7
                    pi_ = 6 if i == 0 else 0
                    for sch in range(nt):
                        MM(psb(pr), cft[i][:, sch, :], hs[:, sch, :], sch == 0, sch == nt - 1,
                           [f"cft{i}", f"hs{sch}"], [f"ps{pr}"])
                    for sch in range(nt):
                        MM(psb(pi_), sft[i][:, sch, :], hdd[:, sch, :], sch == 0, sch == nt - 1,
                           [f"sft{i}", f"hd{sch}"], [f"ps{pi_}"])
                    for ri, pbank in ((0, pr), (1, pi_)):
                        gs = gst[ng % 4]
                        TTo("dve", gs, psb(pbank), rn, ALU.mult, [f"ps{pbank}", "rn"], [f"gst{ng % 4}"])
                        DMA("sp", gspec[L][o, ri, kt * 128:(kt + 1) * 128, chf * 512:(chf + 1) * 512], gs,
                            [f"gst{ng % 4}"], [f"gspec{o}_{ri}_{kt}_{chf}"])
                        ng += 1
        S.barrier()

    def stage_hyena(l, t0, L, chf):
        A = Arena(PERSIST)
        nt = L // 128
        QT = min(256, L)
        scw = A.f32(36)
        scb = A.f32(12)
        hbias = A.f32(8)
        vT = A.bf16(4 * L).rearrange("p (c n) -> p c n", n=L)
        xm = A.bf16(4 * L).rearrange("p (c n) -> p c n", n=L)
        z1 = A.bf16(4 * L).rearrange("p (c n) -> p c n", n=L)
        ztm = A.bf16(nt * 512).rearrange("p (t n) -> p t n", n=512)
        Yr = A.bf16(nt * 512).rearrange("p (t n) -> p t n", n=512)
        Yi = A.bf16(nt * 512).rearrange("p (t n) -> p t n", n=512)
        rawc = [A.bf16(L) for _ in range(2)]
        acc = [A.f32(L) for _ in range(1)]
        cft = [A.bf16(nt * 128).rearrange("p (t n) -> p t n", n=128) for _ in range(2)]
        sft = [A.bf16(nt * 128).rearrange("p (t n) -> p t n", n=128) for _ in range(2)]
        gr = [A.bf16(512) for _ in range(2)]
        gi = [A.bf16(512) for _ in range(2)]
        pa = [A.f32(512) for _ in range(2)]
        cit = A.bf16(nt * QT).rearrange("p (t n) -> p t n", n=QT)
        sit = A.bf16(nt * QT).rearrange("p (t n) -> p t n", n=QT)
        yt = [A.f32(QT) for _ in range(2)]
        hg = [A.bf16(QT) for _ in range(2)]
        z2 = [A.bf16(QT) for _ in range(2)]
        ost = [A.bf16(QT) for _ in range(2)]
        c0 = chf * 512
        for part in range(3):
            for tap in range(3):
                DMA("sp", scw[:, part * 12:(part + 1) * 12].rearrange("p (c t) -> p c t", t=3)[:, :, tap],
                    hy_sw[l, tap, part * 1024 + c0:part * 1024 + c0 + 512].rearrange("(c p) -> p c", p=128), [],
                    ["scw"], slow=True)
            DMA("sp", scb[:, part * 4:(part + 1) * 4],
                hy_sb[l, part * 1024 + c0:part * 1024 + c0 + 512].rearrange("(c p) -> p c", p=128), [], ["scb"],
                slow=True)
        for o in range(2):
            DMA("sp", hbias[:, o * 4:(o + 1) * 4], hy_bias[l, o, c0:c0 + 512].rearrange("(c p) -> p c", p=128), [],
                ["hbias"], slow=True)
        nr = {"n": 0}

        def short_conv(part, dst, dkey):
            base = (C_HV, C_HX1, C_HX2)[part] + c0
            for c in range(4):
                i = nr["n"] % 2
                nr["n"] += 1
                R, AC = rawc[i], acc[0]
                DMA("sp", R, projT[base + c * 128:base + (c + 1) * 128, t0:t0 + L], [], [f"rawc{i}"])
                wv = scw[:, part * 12 + c * 3:part * 12 + c * 3 + 3]
                TS("dve", AC, R, wv[:, 1:2], scb[:, part * 4 + c:part * 4 + c + 1], ALU.mult, ALU.add,
                   [f"rawc{i}", "scw", "scb"], ["acc0"])
                STT(AC[:, 1:L], R[:, 0:L - 1], wv[:, 0:1], AC[:, 1:L], ALU.mult, ALU.add, [f"rawc{i}", "scw", "acc0"],
                    ["acc0"])
                STT(AC[:, 0:L - 1], R[:, 1:L], wv[:, 2:3], AC[:, 0:L - 1], ALU.mult, ALU.add,
                    [f"rawc{i}", "scw", "acc0"], ["acc0"])
                CP("pool", dst[:, c, :], AC, ["acc0"], [f"{dkey}{c}"])

        def conv(o, zin, zkey, last):
            for tch in range(nt):
                pb = tch % 2
                pv = psb16(pb)[:, 0:512]
                for c in range(4):
                    TR(pv[:, c * 128:(c + 1) * 128], zin[:, c, tch * 128:(tch + 1) * 128], ident_bf,
                       [f"{zkey}{c}", "ident_bf"], [f"ps{pb}"])
                CP("act", ztm[:, tch, :], pv, [f"ps{pb}"], [f"ztm{tch}"])
            for kt in range(nt):
                i = kt % 2
                DMA("pool", cft[i], tabs[f"hy_cf{L}"][:, kt * 128:(kt + 1) * 128].rearrange(
                    "(t p) n -> p t n", p=128), [], [f"cft{i}"])
                DMA("pool", sft[i], tabs[f"hy_sf{L}"][:, kt * 128:(kt + 1) * 128].rearrange(
                    "(t p) n -> p t n", p=128), [], [f"sft{i}"])
                DMA("sp", gr[i], gspec[L][o, 0, kt * 128:(kt + 1) * 128, c0:c0 + 512], [], [f"gr{i}"])
                DMA("sp", gi[i], gspec[L][o, 1, kt * 128:(kt + 1) * 128, c0:c0 + 512], [], [f"gi{i}"])
                pr, pi_ = 2 + 2 * i, 3 + 2 * i
                for sch in range(nt):
                    MM(psb(pr), cft[i][:, sch, :], ztm[:, sch, :], sch == 0, sch == nt - 1,
                       [f"cft{i}", f"ztm{sch}"], [f"ps{pr}"])
                for sch in range(nt):
                    MM(psb(pi_), sft[i][:, sch, :], ztm[:, sch, :], sch == 0, sch == nt - 1,
                       [f"sft{i}", f"ztm{sch}"], [f"ps{pi_}"])
                TTo("dve", pa[0], psb(pr), gr[i], ALU.mult, [f"ps{pr}", f"gr{i}"], ["pa0"])
                TTo("dve", pa[1], psb(pi_), gi[i], ALU.mult, [f"ps{pi_}", f"gi{i}"], ["pa1"])
                TTo("pool", Yr[:, kt, :], pa[0], pa[1], ALU.subtract, ["pa0", "pa1"], [f"Yr{kt}"])
                TTo("dve", pa[0], psb(pr), gi[i], ALU.mult, [f"ps{pr}", f"gi{i}"], ["pa0"])
                TTo("dve", pa[1], psb(pi_), gr[i], ALU.mult, [f"ps{pi_}", f"gr{i}"], ["pa1"])
                TTo("pool", Yi[:, kt, :], pa[0], pa[1], ALU.add, ["pa0", "pa1"], [f"Yi{kt}"])
            n = 0
            for tq in range(L // QT):
                DMA("pool", cit, tabs[f"hy_cft{L}"][:, tq * QT:(tq + 1) * QT].rearrange("(t p) n -> p t n", p=128),
                    [], ["cit"])
                DMA("pool", sit, tabs[f"hy_sft{L}"][:, tq * QT:(tq + 1) * QT].rearrange("(t p) n -> p t n", p=128),
                    [], ["sit"])
                for c in range(4):
                    pb = 6 + n % 2
                    i = n % 2
                    n += 1
                    for kch in range(nt):
                        MM(psb(pb)[:, 0:QT], Yr[:, kch, c * 128:(c + 1) * 128], cit[:, kch, :], kch == 0, False,
                           [f"Yr{kch}", "cit"], [f"ps{pb}"])
                    for kch in range(nt):
                        MM(psb(pb)[:, 0:QT], Yi[:, kch, c * 128:(c + 1) * 128], sit[:, kch, :], False, kch == nt - 1,
                           [f"Yi{kch}", "sit"], [f"ps{pb}"])
                    ts_ = slice(tq * QT, (tq + 1) * QT)
                    STT(yt[i], zin[:, c, ts_], hbias[:, o * 4 + c:o * 4 + c + 1], psb(pb)[:, 0:QT], ALU.mult, ALU.add,
                        [f"{zkey}{c}", "hbias", f"ps{pb}"], [f"yt{i}"])
                    if not last:
                        TTo("dve", z1[:, c, ts_], yt[i], xm[:, c, ts_], ALU.mult, [f"yt{i}", f"xm{c}"], [f"z1{c}"])
                    else:
                        gbase = C_HG + c0 + c * 128
                        DMA("sp", hg[i], projT[gbase:gbase + 128, t0 + tq * QT:t0 + (tq + 1) * QT], [], [f"hg{i}"])
                        TTo("dve", z2[i], yt[i], xm[:, c, ts_], ALU.mult, [f"yt{i}", f"xm{c}"], [f"z2{i}"])
                        TTo("pool", ost[i], z2[i], hg[i], ALU.mult, [f"z2{i}", f"hg{i}"], [f"ost{i}"])
                        r0 = 3072 + c0 + c * 128
                        DMA("sp", bT[r0:r0 + 128, t0 + tq * QT:t0 + (tq + 1) * QT], ost[i], [f"ost{i}"],
                            [f"bT_h{c}_{tq}"])

        short_conv(0, vT, "vT")
        short_conv(1, xm, "xm")
        conv(0, vT, "vT", False)
        short_conv(2, xm, "xm")
        conv(1, z1, "z1", True)
        S.barrier()

    def stage_out(l, kind, off, j):
        t0 = tok0(kind, off)
        A = Arena(PERSIST)
        bTt = A.bf16(KC * 512).rearrange("p (k n) -> p k n", n=512)
        mT = A.bf16(KC * 512).rearrange("p (k n) -> p k n", n=512)
        base2 = A.off
        wblk = [A.bf16(KC * 512).rearrange("p (k n) -> p k n", n=512) for _ in range(2)]
        sg = [[A.bf16(512) for _ in range(3)] for _ in range(2)]
        ta = [A.f32(512) for _ in range(2)]
        tb = [A.f32(512) for _ in range(2)]
        tc_ = [A.f32(512) for _ in range(2)]
        DMA("sp", wblk[0], wb_o[l][0], [f"wbo{l}_0a", f"wbo{l}_0f", f"wbo{l}_0h"], ["wblk0"])
        DMA("sp", bTt, bT[:, t0:t0 + 512].rearrange("(k p) t -> p k t", p=128), [], ["bTt"])

        def load_sg(nn):
            cq = nn * 128
            for bi, cg in enumerate((C_GA, C_GF, C_GH)):
                DMA("sp", sg[nn % 2][bi], projT[cg + cq:cg + cq + 128, t0:t0 + 512], [], [f"sg{nn % 2}_{bi}"])

        load_sg(0)
        n = 0
        for cb in range(8):
            wb = wblk[cb % 2]
            if cb + 1 < 8:
                DMA("sp", wblk[(cb + 1) % 2], wb_o[l][cb + 1],
                    [f"wbo{l}_{cb + 1}a", f"wbo{l}_{cb + 1}f", f"wbo{l}_{cb + 1}h"], [f"wblk{(cb + 1) % 2}"])
            for q in range(4):
                c0 = cb * 512 + q * 128
                i = n % 2
                n += 1
                if n < 32:
                    load_sg(n)
                pbs = (2 + 3 * i, 3 + 3 * i, 4 + 3 * i)
                for bi, (k0, k1) in enumerate(((0, 16), (16, 24), (24, 32))):
                    for k in range(k0, k1):
                        MM(psb(pbs[bi]), wb[:, k, q * 128:(q + 1) * 128], bTt[:, k, :], k == k0, k == k1 - 1,
                           [f"wblk{cb % 2}", "bTt"], [f"ps{pbs[bi]}"])
                TTo("dve", ta[i], psb(pbs[0]), sg[i][0], ALU.mult, [f"ps{pbs[0]}", f"sg{i}_0"], [f"ta{i}"])
                TTo("dve", tb[i], psb(pbs[1]), sg[i][1], ALU.mult, [f"ps{pbs[1]}", f"sg{i}_1"], [f"tb{i}"])
                TTo("dve", tc_[i], psb(pbs[2]), sg[i][2], ALU.mult, [f"ps{pbs[2]}", f"sg{i}_2"], [f"tc{i}"])
                TTo("pool", ta[i], ta[i], tb[i], ALU.add, [f"ta{i}", f"tb{i}"], [f"ta{i}"])
                TTo("pool", mT[:, cb * 4 + q, :], ta[i], tc_[i], ALU.add, [f"ta{i}", f"tc{i}"], [f"mT{cb * 4 + q}"])
        S.barrier()
        oT = bTt
        n = 0
        DMA("sp", wblk[0], wb_out[l][0], [f"wbout{l}_0"], ["wblk0"])
        for cb in range(8):
            wb = wblk[cb % 2]
            if cb + 1 < 8:
                DMA("sp", wblk[(cb + 1) % 2], wb_out[l][cb + 1], [f"wbout{l}_{cb + 1}"], [f"wblk{(cb + 1) % 2}"])
            for q in range(4):
                pb = 2 + n % 6
                n += 1
                for k in range(KC):
                    MM(psb(pb), wb[:, k, q * 128:(q + 1) * 128], mT[:, k, :], k == 0, k == KC - 1,
                       [f"wblk{cb % 2}", f"mT{k}"], [f"ps{pb}"])
                CP("act" if n % 2 else "dve", oT[:, cb * 4 + q, :], psb(pb), [f"ps{pb}"], [f"oT{cb * 4 + q}"])
        S.barrier()
        B = Arena(base2)
        xf = [B.f32(D) for _ in range(2)]
        tf = [B.f32(D) for _ in range(2)]
        gg = B.f32(D)
        junk = B.bf16(D)
        otm = B.bf16(D)
        ssq = B.f32(4)
        rstd = B.f32(4)
        DMA("sp", gg, ggb[j], [], ["gg"])
        for s in range(4):
            i = s % 2
            DMA("sp", xf[i], x_rows(l, kind, off + s * 128, 128), [], [f"xf{i}"])
            pv = ps[:, 4 * i:4 * i + 4, :].rearrange("p b n -> p (b n)").bitcast(BF16)[:, 0:D]
            pkeys = [f"ps{4 * i + b}" for b in range(4)]
            for k in range(KC):
                TR(pv[:, k * 128:(k + 1) * 128], oT[:, k, s * 128:(s + 1) * 128], ident_bf, [f"oT{k}", "ident_bf"],
                   pkeys)
            ACT(junk, pv, AF.Square, pkeys, ["junk", f"ssq{s}"], accum=ssq[:, s:s + 1])
            RSQRT_ACT(rstd[:, s:s + 1], ssq[:, s:s + 1], 1.0 / D, [f"ssq{s}"], [f"rstd{s}"])
            CP("act", otm, pv, pkeys, ["otm"])
            STT(tf[i], otm, rstd[:, s:s + 1], gg, ALU.mult, ALU.mult, ["otm", f"rstd{s}", "gg"], [f"tf{i}"])
            TTo("pool", tf[i], tf[i], xf[i], ALU.add, [f"tf{i}", f"xf{i}"], [f"tf{i}"])
            DMA("sp", y_rows(l, kind, off + s * 128, 128), tf[i], [f"tf{i}"], [f"y_{kind}_{off}_{s}"])
        S.barrier()

    stages = getattr(cfg, "stages", None)

    def on(name):
        return stages is None or name in stages

    for l in range(DEPTH):
        if on("mod"):
            stage_mod(l)
        if on("proj"):
            for (kind, off, j) in tiles:
                stage_proj(l, kind, off, j)
        if on("filt"):
            for L in Ls:
                stage_filters(l, L)
        for s in range(NPS):
            if on("attn_p"):
                stage_attn(l, s * LP, LP, False, s)
            if on("fnet_p"):
                stage_fnet(l, s * LP, LP)
            if on("hy_p"):
                for chf in range(2):
                    stage_hyena(l, s * LP, LP, chf)
        if on("attn_s"):
            stage_attn(l, TP, LS, True, None)
        if on("fnet_s"):
            stage_fnet(l, TP, LS)
        if on("hy_s"):
            for chf in range(2):
                stage_hyena(l, TP, LS, chf)
        if cfg.debug and l == 0:
            for r0 in range(0, IN_W, 512):
                DMA("sp", dbg_proj[r0:r0 + 512, :], projT[r0:r0 + 512, :], [], [f"dbg_proj{r0}"])
            for r0 in range(0, D, 512):
                DMA("sp", dbg_bT[r0:r0 + 512, :], bT[r0:r0 + 512, :], [], [f"dbg_bT{r0}"])
            S.barrier()
        if on("out"):
            for (kind, off, j) in tiles:
                stage_out(l, kind, off, j)

    sem_cms = [nc.semaphore(f"e_{e}") for e in Sched.ENGS] + [nc.semaphore(f"d_{i}") for i in range(Sched.NDMA)]
    sems = [c.__enter__() for c in sem_cms]
    eng_sems = {e: sems[i] for i, e in enumerate(Sched.ENGS)}
    dma_sems = sems[len(Sched.ENGS):]
    with nc.Block() as block:
        S.emit(nc, block, eng_sems, dma_sems)
    nops = {e: len(S.ops[e]) for e in Sched.ENGS}
    return nc, nops


NCORES = 8


def kernel(x_prompt, x_sample, cache_k, cache_v, c, c_ctx, w_mod, b_mod, g_pre, w_in, q_norm, k_norm,
           hy_short_w, hy_short_b, hy_ffn_w1, hy_ffn_b1, hy_ffn_w2, hy_ffn_b2, hy_ffn_w3, hy_ffn_b3,
           hy_sin_freq, hy_bias, w_attn_o, w_fnet_o, w_hy_o, w_out, g_post):
    f = lambda a: np.ascontiguousarray(np.asarray(a), dtype=np.float32)
    x_prompt, x_sample, cache_k, cache_v, c, c_ctx = map(f, (x_prompt, x_sample, cache_k, cache_v, c, c_ctx))
    B, SEQ, _ = x_prompt.shape
    DB, LS, _ = x_sample.shape
    depth = w_mod.shape[0]
    nps = B // NCORES
    cfg = Cfg(ncores=NCORES, depth=depth, nps=nps, ls=LS)
    nc, _ = build_program(cfg)
    tabs = make_tables(LS)
    shared = {
        "w_mod": f(w_mod), "b_mod": f(b_mod), "g_pre": f(g_pre), "w_in": f(w_in), "q_norm": f(q_norm),
        "k_norm": f(k_norm), "hy_short_w": f(hy_short_w), "hy_short_b": f(hy_short_b), "hy_ffn_w1": f(hy_ffn_w1),
        "hy_ffn_b1": f(hy_ffn_b1), "hy_ffn_w2": f(hy_ffn_w2), "hy_ffn_b2": f(hy_ffn_b2), "hy_ffn_w3": f(hy_ffn_w3),
        "hy_ffn_b3": f(hy_ffn_b3), "hy_sin_freq": f(hy_sin_freq), "hy_bias": f(hy_bias), "w_attn_o": f(w_attn_o),
        "w_fnet_o": f(w_fnet_o), "w_hy_o": f(w_hy_o), "w_out": f(w_out), "g_post": f(g_post),
    }
    shared.update(tabs)
    in_maps = []
    for core in range(NCORES):
        b = core * DB // NCORES
        m = dict(shared)
        m["xp"] = x_prompt[core * nps:(core + 1) * nps].reshape(nps * SEQ, D)
        m["xs"] = x_sample[b]
        m["ck"] = cache_k[b].reshape(depth, PAST, NKV * HD)
        m["cv"] = cache_v[b].reshape(depth, PAST, NKV * HD)
        m["cond"] = np.stack([c_ctx, c[b]], axis=0)
        in_maps.append(m)
    res = run_bass_kernel_spmd(nc, in_maps, core_ids=list(range(NCORES)))
    r = res.results
    y_prompt = np.concatenate([r[k]["y_p"].reshape(nps, SEQ, D) for k in range(NCORES)], axis=0)
    y_sample = np.stack([r[(bb * NCORES) // DB]["y_s"] for bb in range(DB)], axis=0)
    nk = np.concatenate([r[k]["nck"].reshape(nps, depth, SEQ, NKV, HD) for k in range(NCORES)], axis=0)
    nv = np.concatenate([r[k]["ncv"].reshape(nps, depth, SEQ, NKV, HD) for k in range(NCORES)], axis=0)
    return (y_prompt.astype(np.float32), y_sample.astype(np.float32), nk.astype(np.float32), nv.astype(np.float32))
```
